# Optimizing a Trainium2 kernel written in Bass

```python
import jax, jax.numpy as jnp
from jax import lax
import numpy as np

D_MODEL = 1024
BATCH = 16
SEQ = 2048
DEPTH = 1

MIX_A = D_MODEL // 2
MIX_B = D_MODEL - MIX_A
CHUNK = 128
A_GROUPS = 4
A_GROUP_DIM = MIX_A // A_GROUPS
B_HEAD_DIM = 64
B_HEADS = MIX_B // B_HEAD_DIM
LORA_W = 64
LORA_A = 64
LORA_G = 160
RWKV_COLS = 3 * MIX_B + LORA_W + LORA_A + LORA_G
IN_COLS = 2 * MIX_A + RWKV_COLS
FFN_HIDDEN = ((8 * D_MODEL + 3 * 256 - 1) // (3 * 256)) * 256
ALPHA = (2 * DEPTH) ** 0.25
BETA = (8 * DEPTH) ** -0.25
LN_EPS = 1e-5
GN_EPS = 64e-5

kernel_name = "hybrid_gmlp_rwkv7_deepnorm_adaln"


def layer_norm(x, g, b, eps=LN_EPS):
    xf = x.astype(jnp.float32)
    mu = jnp.mean(xf, axis=-1, keepdims=True)
    var = jnp.mean(jnp.square(xf - mu), axis=-1, keepdims=True)
    y = (xf - mu) * lax.rsqrt(var + eps)
    return (y * g.astype(jnp.float32) + b.astype(jnp.float32)).astype(x.dtype)


def token_shift(p):
    return jnp.pad(p, ((0, 0), (1, 0), (0, 0)))[:, :-1]


def gmlp_spatial_gating(pu, pv, ln_g, ln_b, w_s, b_s):
    B, S, _ = pu.shape
    nc = S // CHUNK
    u = jax.nn.gelu(pu)
    v = layer_norm(jax.nn.gelu(pv), ln_g, ln_b)
    vc = v.reshape(B, nc, CHUNK, A_GROUPS, A_GROUP_DIM)
    mask = jnp.tril(jnp.ones((CHUNK, CHUNK), dtype=w_s.dtype))
    s = jnp.einsum('gts,bnsgd->bntgd', w_s * mask, vc) + b_s.T[None, None, :, :, None]
    return u * s.reshape(B, S, MIX_A)


def _rwkv7_step(S, inp):
    r, w, k, v, a, b = inp
    sa = jnp.einsum('bhij,bhj->bhi', S, a)
    S = S * w[:, :, None, :] + sa[..., None] * b[:, :, None, :] + v[..., None] * k[:, :, None, :]
    y = jnp.einsum('bhij,bhj->bhi', S, r)
    return S, y


def rwkv7_time_mix(p, mu, w0, w_up, a0, a_up, g_up, k_k, k_a, r_k, lnx_g, lnx_b):
    B, S, _ = p.shape
    p = p + (token_shift(p) - p) * mu
    r, k, v = p[..., :MIX_B], p[..., MIX_B:2 * MIX_B], p[..., 2 * MIX_B:3 * MIX_B]
    o = 3 * MIX_B
    wd = p[..., o:o + LORA_W]
    ad = p[..., o + LORA_W:o + LORA_W + LORA_A]
    gd = p[..., o + LORA_W + LORA_A:]
    f32 = jnp.float32
    wlog = -jax.nn.softplus(-(w0 + jnp.tanh(wd) @ w_up).astype(f32)) - 0.5
    decay = jnp.exp(-jnp.exp(wlog))
    a = jax.nn.sigmoid((a0 + ad @ a_up).astype(f32))
    g = jax.nn.sigmoid(gd) @ g_up
    rf, kf, vf = r.astype(f32), k.astype(f32), v.astype(f32)
    heads = lambda t: t.reshape(B, S, B_HEADS, B_HEAD_DIM)
    kk = heads(kf * k_k.astype(f32))
    kk = kk / jnp.maximum(jnp.sqrt(jnp.sum(kk * kk, axis=-1, keepdims=True)), 1e-12)
    kf = kf * (1.0 + (a - 1.0) * k_a.astype(f32))
    rh, kh, vh, wh, ah = heads(rf), heads(kf), heads(vf), heads(decay), heads(a)
    xs = tuple(jnp.swapaxes(t, 0, 1) for t in (rh, wh, kh, vh, -kk, kk * ah))
    S0 = jnp.zeros((B, B_HEADS, B_HEAD_DIM, B_HEAD_DIM), f32)
    _, y = lax.scan(_rwkv7_step, S0, xs)
    y = jnp.swapaxes(y, 0, 1)
    mu_y = jnp.mean(y, axis=-1, keepdims=True)
    var_y = jnp.mean(jnp.square(y - mu_y), axis=-1, keepdims=True)
    y = ((y - mu_y) * lax.rsqrt(var_y + GN_EPS)).reshape(B, S, MIX_B)
    y = y * lnx_g.astype(f32) + lnx_b.astype(f32)
    bonus = jnp.sum(rh * kh * r_k.astype(f32), axis=-1, keepdims=True) * vh
    y = y + bonus.reshape(B, S, MIX_B)
    return (y * g.astype(f32)).astype(p.dtype)


def setup_inputs(seed: int = 0) -> dict:
    key = jax.random.key(seed)
    ks = jax.random.split(key, 32)
    L, D = DEPTH, D_MODEL
    f32 = jnp.float32
    nrm = lambda k, shape, s: jax.random.normal(k, shape, f32) * s
    return {
        "x": nrm(ks[0], (BATCH, SEQ, D), 1.0),
        "c": nrm(ks[1], (BATCH, D), 1.0),
        "emb_ln_g": 1.0 + nrm(ks[2], (D,), 0.05),
        "emb_ln_b": nrm(ks[3], (D,), 0.02),
        "w_ada": nrm(ks[4], (L, D, 6 * D), 0.5 * D ** -0.5),
        "b_ada": nrm(ks[5], (L, 6 * D), 0.02),
        "w_in": nrm(ks[6], (L, D, IN_COLS), D ** -0.5),
        "mu_shift": jax.random.uniform(ks[7], (L, RWKV_COLS), f32),
        "sg_ln_g": 1.0 + nrm(ks[8], (L, MIX_A), 0.05),
        "sg_ln_b": nrm(ks[9], (L, MIX_A), 0.02),
        "w_s": nrm(ks[10], (L, A_GROUPS, CHUNK, CHUNK), CHUNK ** -0.5),
        "b_s": 1.0 + nrm(ks[11], (L, A_GROUPS, CHUNK), 0.1),
        "w0": jax.random.uniform(ks[12], (L, MIX_B), f32, -6.0, 1.0),
        "w_up": nrm(ks[13], (L, LORA_W, MIX_B), LORA_W ** -0.5),
        "a0": nrm(ks[14], (L, MIX_B), 0.1),
        "a_up": nrm(ks[15], (L, LORA_A, MIX_B), LORA_A ** -0.5),
        "g_up": nrm(ks[16], (L, LORA_G, MIX_B), LORA_G ** -0.5),
        "k_k": 0.85 + nrm(ks[17], (L, MIX_B), 0.05),
        "k_a": 1.0 + nrm(ks[18], (L, MIX_B), 0.05),
        "r_k": nrm(ks[19], (L, B_HEADS, B_HEAD_DIM), 0.1),
        "lnx_g": 1.0 + nrm(ks[20], (L, MIX_B), 0.05),
        "lnx_b": nrm(ks[21], (L, MIX_B), 0.02),
        "w_out": nrm(ks[22], (L, D, D), BETA * D ** -0.5),
        "ln1_g": 1.0 + nrm(ks[23], (L, D), 0.05),
        "ln1_b": nrm(ks[24], (L, D), 0.02),
        "w_ffn_in": nrm(ks[25], (L, D, 2 * FFN_HIDDEN), D ** -0.5),
        "w_ffn_out": nrm(ks[26], (L, FFN_HIDDEN, D), BETA * FFN_HIDDEN ** -0.5),
        "ln2_g": 1.0 + nrm(ks[27], (L, D), 0.05),
        "ln2_b": nrm(ks[28], (L, D), 0.02),
    }


def reference(x, c, emb_ln_g, emb_ln_b, w_ada, b_ada, w_in, mu_shift, sg_ln_g, sg_ln_b, w_s, b_s,
              w0, w_up, a0, a_up, g_up, k_k, k_a, r_k, lnx_g, lnx_b, w_out, ln1_g, ln1_b,
              w_ffn_in, w_ffn_out, ln2_g, ln2_b):
    x = layer_norm(x, emb_ln_g, emb_ln_b)
    cs = jax.nn.silu(c)
    for l in range(DEPTH):
        mod = cs @ w_ada[l] + b_ada[l]
        sh1, sc1, gt1, sh2, sc2, gt2 = [m[:, None, :] for m in jnp.split(mod, 6, axis=-1)]
        h = x * (1.0 + sc1) + sh1
        p = h @ w_in[l]
        y_a = gmlp_spatial_gating(p[..., :MIX_A], p[..., MIX_A:2 * MIX_A],
                                  sg_ln_g[l], sg_ln_b[l], w_s[l], b_s[l])
        y_b = rwkv7_time_mix(p[..., 2 * MIX_A:], mu_shift[l], w0[l], w_up[l], a0[l], a_up[l],
                             g_up[l], k_k[l], k_a[l], r_k[l], lnx_g[l], lnx_b[l])
        mix = jnp.concatenate([y_a, y_b], axis=-1) @ w_out[l]
        x = layer_norm(ALPHA * x + gt1 * mix, ln1_g[l], ln1_b[l])
        h = x * (1.0 + sc2) + sh2
        gu = h @ w_ffn_in[l]
        f = (jax.nn.silu(gu[..., :FFN_HIDDEN]) * gu[..., FFN_HIDDEN:]) @ w_ffn_out[l]
        x = layer_norm(ALPHA * x + gt2 * f, ln2_g[l], ln2_b[l])
    return x
```

```python
import numpy as np
import ml_dtypes
import concourse.bass as bass
import concourse.mybir as mybir
from concourse.bass_utils import run_bass_kernel_spmd

F32 = mybir.dt.float32
BF16 = mybir.dt.bfloat16
AF = mybir.ActivationFunctionType
ALU = mybir.AluOpType
AX = mybir.AxisListType

SAME_ENGINE_SYNC = True
EMBED_WAIT = True
N_HW_CH = 8
N_SW_CH = 4
N_DMA_CH = N_HW_CH + N_SW_CH
SB_BASE = 16512
SB_END = 229376

D = 1024
SEQ = 2048
NSEQ = 2
NT = 256
NJ = NT // 128
NCH = NT // 64
NSTEP = SEQ // NT
IN_COLS = 2848
FFN = 2816
NKF = FFN // 128
ALPHA = 2.0 ** 0.25
C0 = float(np.exp(-0.5))
LN_EPS = 1e-5
GN_EPS = 64e-5

PC_MU, PC_W0, PC_A0, PC_KK, PC_KA, PC_RK, PC_LG, PC_LB, PC_BADA, PC_CT, PC_N = 0, 19, 23, 27, 31, 35, 39, 43, 47, 95, 111
CO_MASK, CO_RESET, CO_ID, CO_ONES, CO_TRIU, CO_N = 0, 512, 768, 896, 1024, 1152


class Buf:
    __slots__ = ("name", "last_w", "readers")

    def __init__(self, name):
        self.name = name
        self.last_w = None
        self.readers = []


class Op:
    __slots__ = ("eng", "emit", "deps", "signal", "pos", "sigval", "is_dma", "ch", "chval", "tag", "know", "gid")

    def __init__(self, eng, emit, is_dma=False):
        self.eng = eng
        self.emit = emit
        self.deps = []
        self.signal = False
        self.pos = -1
        self.sigval = -1
        self.is_dma = is_dma
        self.ch = None
        self.chval = 0
        self.tag = None
        self.know = None
        self.gid = 0


class T:
    def __init__(self, h, buf):
        self.h = h
        self.buf = buf

    def __getitem__(self, key):
        return V(self.h[key], (self.buf,))


class TA:
    def __init__(self, h, bufs):
        self.h = h
        self.bufs = bufs

    def __getitem__(self, key):
        return V(self.h[key], self.bufs)


class V:
    __slots__ = ("ap", "bufs")

    def __init__(self, ap, bufs):
        self.ap = ap
        self.bufs = bufs

    def __getitem__(self, key):
        return V(self.ap[key], self.bufs)

    def re(self, s, **kw):
        return V(self.ap.rearrange(s, **kw), self.bufs)

    def bc(self, shape):
        return V(self.ap.to_broadcast(shape), self.bufs)


class Sched:
    ENGS = ("pe", "act", "dve", "pool", "sp")

    def __init__(self, nc):
        self.nc = nc
        self.ops = {e: [] for e in self.ENGS}
        self.seen = {e: {} for e in self.ENGS}
        self.ch_last = [None] * N_DMA_CH
        self.ch_cnt = [0] * N_DMA_CH
        self.ch_rr = {"hw": 0, "sw": 0}
        self.n_t = 0
        self.off = SB_BASE
        self.tag = ""
        self.gid = 0

    def mark(self):
        return self.off

    def reset(self, off):
        self.off = off

    def sb(self, name, shape, dtype, buf=None):
        self.n_t += 1
        nm = f"{name}_{self.n_t}"
        esz = 4 if dtype == F32 else 2
        n = 1
        for s in shape[1:]:
            n *= s
        nbytes = (n * esz + 63) // 64 * 64
        off = self.off
        self.off += nbytes
        assert self.off <= SB_END, f"SBUF overflow at {name}: {self.off}"
        h = self.nc.alloc_sbuf_tensor_at(nm, list(shape), dtype, offset=off)
        t = T(h, buf if buf is not None else Buf(nm))
        t.off = off
        return t

    def alias(self, name, shape, dtype, over):
        self.n_t += 1
        nm = f"{name}_{self.n_t}"
        h = self.nc.alloc_sbuf_tensor_at(nm, list(shape), dtype, offset=over[0].off)
        t = TA(h, tuple(o.buf for o in over))
        t.off = over[0].off
        return t

    def ps(self, name, shape, dtype=F32):
        self.n_t += 1
        nm = f"{name}_{self.n_t}"
        h = self.nc.alloc_psum_tensor(nm, list(shape), dtype)
        return T(h, Buf(nm))

    def _add(self, op, reads, writes):
        E = op.eng
        deps = []
        raw = set()
        for v in reads:
            for b in v.bufs:
                if b.last_w is not None:
                    deps.append(b.last_w)
                    raw.add(id(b.last_w))
        for v in writes:
            for b in v.bufs:
                if b.last_w is not None:
                    deps.append(b.last_w)
                deps.extend(b.readers)
        seen = self.seen[E]
        need = {}
        for d in deps:
            if d is op:
                continue
            if d.is_dma:
                key = ("ch", d.ch)
                if seen.get(key, 0) >= d.chval:
                    continue
                if key not in need or need[key].chval < d.chval:
                    need[key] = d
            else:
                if d.eng == E and (E == "pe" or not SAME_ENGINE_SYNC or id(d) not in raw):
                    continue
                key = ("e", d.eng)
                if seen.get(key, -1) >= d.pos:
                    continue
                if key not in need or need[key].pos < d.pos:
                    need[key] = d
        for key, d in sorted(need.items(), key=lambda kv: -kv[1].gid):
            val = d.chval if d.is_dma else d.pos
            if seen.get(key, -1) >= val:
                continue
            seen[key] = val
            if not d.is_dma:
                d.signal = True
                for k2, v2 in d.know.items():
                    if seen.get(k2, -1) < v2:
                        seen[k2] = v2
            op.deps.append(d)
        op.pos = len(self.ops[E])
        op.tag = self.tag
        self.gid += 1
        op.gid = self.gid
        if not op.is_dma:
            op.know = dict(seen)
            op.know[("e", E)] = op.pos
        self.ops[E].append(op)
        for v in reads:
            for b in v.bufs:
                b.readers.append(op)
        for v in writes:
            for b in v.bufs:
                b.last_w = op
                b.readers = []
        return op

    def op(self, eng, emit, reads=(), writes=()):
        return self._add(Op(eng, emit), list(reads), list(writes))

    def dma(self, q, out, in_, **kw):
        op = Op(q, lambda e: e.dma_start(out.ap, in_.ap, **kw), is_dma=True)
        if q == "pool":
            c = N_HW_CH + self.ch_rr["sw"]
            self.ch_rr["sw"] = (self.ch_rr["sw"] + 1) % N_SW_CH
        else:
            c = self.ch_rr["hw"]
            self.ch_rr["hw"] = (self.ch_rr["hw"] + 1) % N_HW_CH
        op.ch = c
        self.ch_cnt[c] += 16
        op.chval = self.ch_cnt[c]
        prev = self.ch_last[c]
        self.ch_last[c] = op
        if prev is not None:
            key = ("ch", c)
            if self.seen[q].get(key, 0) < prev.chval:
                self.seen[q][key] = prev.chval
                op.deps.append(prev)
        return self._add(op, [in_], [out])

    def barrier(self):
        lasts = {e: (self.ops[e][-1] if self.ops[e] else None) for e in self.ENGS}
        chl = list(self.ch_last)
        for e in self.ENGS:
            op = Op(e, lambda eng: eng.nop())
            seen = self.seen[e]
            for f in self.ENGS:
                d = lasts[f]
                if f == e or d is None:
                    continue
                k = lasts[f].pos
                while k >= 0 and self.ops[f][k].is_dma:
                    k -= 1
                if k < 0:
                    continue
                d = self.ops[f][k]
                if seen.get(("e", f), -1) < d.pos:
                    seen[("e", f)] = d.pos
                    d.signal = True
                    op.deps.append(d)
            for c, d in enumerate(chl):
                if d is not None and seen.get(("ch", c), 0) < d.chval:
                    seen[("ch", c)] = d.chval
                    op.deps.append(d)
            op.pos = len(self.ops[e])
            self.ops[e].append(op)

    def emit_all(self):
        nc = self.nc
        from contextlib import ExitStack

        for e in self.ENGS:
            n = 0
            for o in self.ops[e]:
                if o.signal and not o.is_dma:
                    n += 1
                    o.sigval = n
        with ExitStack() as st:
            esem = {e: st.enter_context(nc.semaphore(f"s_{e}")) for e in self.ENGS}
            csem = [st.enter_context(nc.semaphore(f"s_ch{c}")) for c in range(N_DMA_CH)]
            block = st.enter_context(nc.Block())

            def run(e, eng):
                for o in self.ops[e]:
                    deps = o.deps
                    last = None
                    if deps and not o.is_dma and EMBED_WAIT:
                        deps, last = deps[:-1], deps[-1]
                    for d in deps:
                        if d.is_dma:
                            eng.wait_ge(csem[d.ch], d.chval)
                        else:
                            eng.wait_ge(esem[d.eng], d.sigval)
                    ins = o.emit(eng)
                    if last is not None:
                        if last.is_dma:
                            ins._wait_ge(csem[last.ch], last.chval)
                        else:
                            ins._wait_ge(esem[last.eng], last.sigval)
                    if o.is_dma:
                        ins.then_inc(csem[o.ch], 16)
                    elif o.signal:
                        ins.then_inc(esem[e], 1)
                if e == "sp":
                    for c in range(N_DMA_CH):
                        if self.ch_cnt[c]:
                            eng.wait_ge(csem[c], self.ch_cnt[c])

            @block.tensor
            def _(eng):
                run("pe", eng)

            @block.scalar
            def _(eng):
                run("act", eng)

            @block.vector
            def _(eng):
                run("dve", eng)

            @block.gpsimd
            def _(eng):
                run("pool", eng)

            @block.sync
            def _(eng):
                run("sp", eng)


def seq(*gens):
    for g in gens:
        yield from g


def par(*gens):
    gens = list(gens)
    while gens:
        for g in list(gens):
            try:
                next(g)
            except StopIteration:
                gens.remove(g)
                continue
            yield


def tagged(S, name, g):
    while True:
        S.tag = name
        try:
            next(g)
        except StopIteration:
            return
        yield


def par_w(main, fill, ratio):
    main_done = fill_done = False
    while not (main_done and fill_done):
        for _ in range(ratio):
            if main_done:
                break
            try:
                next(main)
            except StopIteration:
                main_done = True
                break
            yield
        if not fill_done:
            try:
                next(fill)
            except StopIteration:
                fill_done = True
                continue
            yield


def drive(g):
    for _ in g:
        pass


def build_program():
    nc = bass.Bass("TRN2", target_bir_lowering=False)
    S = Sched(nc)

    def din(name, shape, dt=F32):
        return T(nc.dram_tensor(name, list(shape), dt, kind="ExternalInput"), Buf(name))

    X = din("x", [NSEQ * SEQ, D])
    PCOL = din("pcol", [128, PC_N])
    CONST = din("consts", [128, CO_N])
    WADA = din("w_ada", [D, 6 * D])
    BADA = din("b_ada", [1, 6 * D])
    WIN = din("w_in", [D, IN_COLS])
    WOUT = din("w_out", [D, D])
    WFI = din("w_ffn_in", [D, 2 * FFN])
    WFO = din("w_ffn_out", [FFN, D])
    ROWS = din("rows", [1, 6 * D + 1024])
    BS = din("b_s", [1, 512])
    WST = din("w_sT", [128, 512])
    WUA = din("w_ua", [128, 512])
    GUP = din("g_up", [160, 512])
    out_h = nc.dram_tensor("out", [NSEQ * SEQ, D], F32, kind="ExternalOutput")
    x1s_h = nc.dram_tensor("x1s", [NSEQ * SEQ, D], F32, kind="Internal")
    gt2s_h = nc.dram_tensor("gt2s", [NSEQ * 128, D], F32, kind="Internal")
    gt1s_h = nc.dram_tensor("gt1s", [NSEQ * 128, D], F32, kind="Internal")
    NBLK = NSEQ * SEQ // 128
    OUTB = [T(out_h, Buf(f"out{i}")) for i in range(NBLK)]
    X1SB = [T(x1s_h, Buf(f"x1s{i}")) for i in range(NBLK)]
    GT2S = T(gt2s_h, Buf("gt2s"))
    GT1S = T(gt1s_h, Buf("gt1s"))

    def act(out, in_, func, bias=None, scale=None):
        reads = [in_]
        kw = {}
        if isinstance(bias, V):
            reads.append(bias)
            kw["bias"] = bias.ap
        elif bias is not None:
            kw["bias"] = bias
        if isinstance(scale, V):
            reads.append(scale)
            kw["scale"] = scale.ap
        elif scale is not None:
            kw["scale"] = scale
        S.op("act", lambda e: e.activation(out.ap, in_.ap, func, **kw), reads, [out])

    def tt(eng, out, a, b, op):
        S.op(eng, lambda e: e.tensor_tensor(out.ap, a.ap, b.ap, op), [a, b], [out])

    def ts(eng, out, a, s1, op0, s2=None, op1=None):
        reads = [a]
        v1 = s1.ap if isinstance(s1, V) else s1
        v2 = s2.ap if isinstance(s2, V) else s2
        if isinstance(s1, V):
            reads.append(s1)
        if isinstance(s2, V):
            reads.append(s2)
        if op1 is None:
            S.op(eng, lambda e: e.tensor_scalar(out.ap, a.ap, v1, None, op0), reads, [out])
        else:
            S.op(eng, lambda e: e.tensor_scalar(out.ap, a.ap, v1, v2, op0, op1), reads, [out])

    def stt(out, a, sc, b, op0, op1):
        reads = [a, b]
        sv = sc.ap if isinstance(sc, V) else sc
        if isinstance(sc, V):
            reads.append(sc)
        S.op("dve", lambda e: e.scalar_tensor_tensor(out.ap, a.ap, sv, b.ap, op0, op1), reads, [out])

    def cp(eng, out, a):
        if eng == "act":
            S.op("act", lambda e: e.activation(out.ap, a.ap, AF.Copy), [a], [out])
        else:
            S.op(eng, lambda e: e.tensor_copy(out.ap, a.ap), [a], [out])

    def recip(out, a):
        S.op("dve", lambda e: e.reciprocal(out.ap, a.ap), [a], [out])

    def mm(out, lhsT, rhs, start=True, stop=True):
        S.op("pe", lambda e: e.matmul(out.ap, lhsT.ap, rhs.ap, start=start, stop=stop), [lhsT, rhs], [out])

    def tr(out, in_):
        S.op("pe", lambda e: e.transpose(out.ap, in_.ap, identbf.ap), [in_, identbf], [out])

    def memset(eng, out, val):
        S.op(eng, lambda e: e.memset(out.ap, val), [], [out])

    PS = S.ps("psum", [128, 4096])
    PSB = PS.h.bitcast(BF16)
    bbuf = [Buf(f"bank{i}") for i in range(8)]

    def pv(i, c0=0, c1=512, p0=0, p1=128):
        return V(PS.h[p0:p1, i * 512 + c0:i * 512 + c1], (bbuf[i],))

    def pvb(i, c0, c1):
        return V(PSB[:, i * 1024 + c0:i * 1024 + c1], (bbuf[i],))

    def pvm(i0, n):
        return V(PS.h[:, i0 * 512:(i0 + n) * 512], tuple(bbuf[i0:i0 + n]))

    def pvmb(i0, n, c0, c1):
        return V(PSB[:, i0 * 1024 + c0:i0 * 1024 + c1], tuple(bbuf[i0:i0 + n]))

    rr = [0]

    def nb():
        i = rr[0]
        rr[0] = (rr[0] + 1) % 4
        return i

    def nb2():
        i = 0 if rr[0] < 2 else 2
        rr[0] = (i + 2) % 4
        return i

    G1 = 4
    g1_3d = lambda c0, c1: V(PS.h[:, G1 * 512:(G1 + 4) * 512].rearrange("p (c x) -> p c x", x=512)[:, :, c0:c1], tuple(bbuf[G1:G1 + 4]))

    consts = S.sb("consts", [128, CO_N], F32)
    pcol = S.sb("pcol", [128, PC_N], F32)
    cst = S.sb("cst", [128, 8], F32)
    omm = S.sb("omm", [128, 19], F32)
    hw0 = S.sb("hw0", [128, 4], F32)
    ha0 = S.sb("ha0", [128, 4], F32)
    hka = S.sb("hka", [128, 4], F32)
    omhka = S.sb("omhka", [128, 4], F32)
    identb = S.sb("identb", [128, 256], BF16)
    WmT = S.sb("WmT", [128, 512], BF16)
    WA = S.sb("WA", [128, 512], BF16)
    GU = S.sb("GU", [128, 512], BF16)
    GU2 = S.sb("GU2", [32, 512], BF16)
    bsrow = S.sb("bsrow", [1, 512], BF16)
    onesrow = S.sb("onesrow", [1, 128], BF16)
    onesb = S.sb("onesb", [128, NT], BF16)
    modc = S.sb("modc", [128, 64], F32)
    gt1t = S.sb("gt1t", [128, D], F32)
    cst_f = S.sb("cs_f", [128, 16], F32)

    maskall = consts[:, CO_MASK:CO_MASK + 512]
    resetm = consts[:, CO_RESET:CO_RESET + NT]
    bones = consts[:, CO_ONES:CO_ONES + 128]
    triu = consts[:, CO_TRIU:CO_TRIU + 128]
    ident32 = consts[:, CO_ID:CO_ID + 128]
    identbf = identb[:, 0:128]

    S.dma("sp", consts[:], CONST[:])
    S.dma("sp", pcol[:], PCOL[:])
    S.dma("pool", WA[:], WUA[:])
    S.dma("pool", GU[:], GUP[0:128, :])
    S.dma("pool", GU2[:], GUP[128:160, :])
    S.dma("pool", bsrow[:], BS[:])
    memset("pool", cst[:, 0:1], LN_EPS)
    memset("pool", cst[:, 1:2], GN_EPS)
    memset("pool", onesrow[:], 1.0)
    memset("pool", onesb[:], 1.0)
    cp("pool", identb[:, 0:128], ident32)
    cp("pool", identb[:, 128:256], ident32)
    ts("dve", omm[:], pcol[:, PC_MU:PC_MU + 19], -1.0, ALU.mult, 1.0, ALU.add)
    ts("dve", hw0[:], pcol[:, PC_W0:PC_W0 + 4], 0.5, ALU.mult)
    ts("dve", ha0[:], pcol[:, PC_A0:PC_A0 + 4], 0.5, ALU.mult)
    ts("dve", hka[:], pcol[:, PC_KA:PC_KA + 4], 0.5, ALU.mult)
    ts("dve", omhka[:], pcol[:, PC_KA:PC_KA + 4], -0.5, ALU.mult, 1.0, ALU.add)

    mk_w = S.mark()
    w_in = S.sb("w_in", [128, 8 * IN_COLS], BF16)
    w_out = S.sb("w_out", [128, 8 * D], BF16)
    mk_pro = S.mark()
    for kc in range(8):
        S.dma("pool", w_in[:, kc * IN_COLS:(kc + 1) * IN_COLS], WIN[kc * 128:(kc + 1) * 128, :])
    for kc in range(8):
        S.dma("pool", w_out[:, kc * D:(kc + 1) * D], WOUT[kc * 128:(kc + 1) * 128, :])
    wst32 = S.sb("wst32", [128, 512], F32)
    S.dma("sp", wst32[:], WST[:])
    for g in range(4):
        tt("dve", WmT[:, g * 128:(g + 1) * 128], wst32[:, g * 128:(g + 1) * 128], triu, ALU.mult)
    act(cst_f[:], pcol[:, PC_CT:PC_CT + 16], AF.Silu)
    csbc = [S.sb(f"csbc{b}", [128, 8 * 128], F32) for b in range(NSEQ)]
    for b in range(NSEQ):
        for kc in range(8):
            act(csbc[b][:, kc * 128:(kc + 1) * 128], pcol[:, PC_CT + b * 8 + kc:PC_CT + b * 8 + kc + 1].bc([128, 128]), AF.Silu)
    wa_st = [S.sb(f"wa_st{i}", [128, 8 * 1024], F32) for i in range(2)]
    brow = S.sb("brow", [128, 1024], F32)
    gttmp = S.sb("gttmp", [128, D], F32)
    colgrp = {0: 0, 1: 1, 3: 2, 4: 3}
    for g in range(6):
        st = wa_st[g % 2]
        for kc in range(8):
            S.dma("sp" if kc % 2 == 0 else "act", st[:, kc * 1024:(kc + 1) * 1024], WADA[kc * 128:(kc + 1) * 128, g * 1024:(g + 1) * 1024])
        if g in colgrp:
            bi = nb()
            for j in range(8):
                for kc in range(8):
                    mm(pv(bi, j * 2, j * 2 + 2), st[:, kc * 1024 + j * 128:kc * 1024 + (j + 1) * 128],
                       V(cst_f.h[:, :].rearrange("p (b k) -> p k b", b=2)[:, kc, :], (cst_f.buf,)), start=(kc == 0), stop=(kc == 7))
            q = colgrp[g]
            o3 = modc[:, q * 16:(q + 1) * 16].re("p (j b) -> p j b", b=2)
            i3 = pv(bi, 0, 16).re("p (j b) -> p j b", b=2)
            bb = V(pcol.h[:, PC_BADA + g * 8:PC_BADA + g * 8 + 8].unsqueeze(2).to_broadcast([128, 8, 2]), (pcol.buf,))
            tt("dve", o3, i3, bb, ALU.add)
        else:
            S.dma("sp", brow[:], V(BADA.h[0:1, g * 1024:(g + 1) * 1024].to_broadcast([128, 1024]), (BADA.buf,)))
            for b in range(NSEQ):
                for half in range(2):
                    bi = nb()
                    for kc in range(8):
                        mm(pv(bi), csbc[b][:, kc * 128:(kc + 1) * 128], st[:, kc * 1024 + half * 512:kc * 1024 + (half + 1) * 512],
                           start=(kc == 0), stop=(kc == 7))
                    tt("dve", gttmp[:, half * 512:(half + 1) * 512], pv(bi), brow[:, half * 512:(half + 1) * 512], ALU.add)
                S.dma("sp", (GT1S if g == 2 else GT2S)[b * 128:(b + 1) * 128, :], gttmp[:])
    ts("dve", modc[:, 16:32], modc[:, 16:32], 1.0, ALU.add)
    ts("dve", modc[:, 48:64], modc[:, 48:64], 1.0, ALU.add)

    def modcol(q, c, b):
        k = q * 16 + c * 2 + b
        return modc[:, k:k + 1]

    S.barrier()
    S.reset(mk_pro)

    rowsb = S.sb("rowsb", [128, 4 * D + 1024], F32)
    S.dma("sp", rowsb[:, 0:4 * D], V(ROWS.h[0:1, 0:4 * D].to_broadcast([128, 4 * D]), (ROWS.buf,)))
    S.dma("sp", rowsb[:, 4 * D:4 * D + 1024], V(ROWS.h[0:1, 6 * D:6 * D + 1024].to_broadcast([128, 1024]), (ROWS.buf,)))
    embg, embb = rowsb[:, 0:D], rowsb[:, D:2 * D]
    ln1g, ln1b = rowsb[:, 2 * D:3 * D], rowsb[:, 3 * D:4 * D]
    sgg, sgb = rowsb[:, 4 * D:4 * D + 512], rowsb[:, 4 * D + 512:4 * D + 1024]

    S32 = S.sb("S32", [128, 4 * 128], F32)
    S0b = S.sb("S0b", [128, 4 * 128], BF16)
    carry = S.sb("carry", [128, 12], F32)
    carryL = S.sb("carryL", [128, 3], F32)

    xt2 = [[S.sb(f"xt{p}_{j}", [128, D], F32) for j in range(NJ)] for p in range(2)]
    x0b = S.sb("x0b", [128, D], BF16)
    hT2 = [[S.sb(f"hT{p}_{c}", [128, NT], BF16) for c in range(8)] for p in range(2)]
    uT = [S.sb(f"uT{g}", [128, NT], BF16) for g in range(4)]
    vb = [S.sb(f"vb{j}", [128, 512], BF16) for j in range(NJ)]
    mixT = [S.sb(f"mixT{c}", [128, NT], BF16) for c in range(8)]
    Bmu = S.sb("Bmu", [128, NT + 1], F32)
    tw_p = [S.sb(f"tw{p}", [128, NT], BF16) for p in range(2)]
    sgt_p = [S.sb(f"sgt{p}", [128, NT], BF16) for p in range(2)]
    sgt2_p = [S.sb(f"sgt2{p}", [32, NT], BF16) for p in range(2)]
    lnscr = {k: (S.sb("stat" + k, [128, 12], F32), S.sb("mv" + k, [128, 2], F32), S.sb("rstd" + k, [128, 1], F32),
                 S.sb("nmr" + k, [128, 1], F32)) for k in ("x", "g", "o")}
    r1 = S.sb("r1", [128, D], F32)

    def ftile(name):
        return S.sb(name, [128, NT], F32)

    rS, kS, vS, th, tha = [ftile(n) for n in ("rS", "kS", "vS", "th", "tha")]
    cs_, eiP, sq, kk0, kkn = [ftile(n) for n in ("cs", "eiP", "sq", "kk0", "kkn")]
    f1, km, t1, rk = [ftile(n) for n in ("f1", "km", "t1", "rk")]
    ltmp = S.alias("ltmp", [128, NT], F32, [rk])
    gv = S.sb("gv", [128, 512], F32)
    ePs = S.sb("ePs", [128, NCH * 65], F32)
    eP3 = ePs[:].re("p (c t) -> p c t", t=65)
    bdBK = [[S.sb(f"bd{n}{i}", [128, NCH * 128], BF16) for n in ("B", "K", "V")] for i in range(2)]
    NCG = NCH // 2
    LtG = [S.sb(f"Lt{i}", [128, NCG * 512], BF16) for i in range(2)]
    ABoffG = [S.sb(f"ABoff{i}", [128, NCG * 256], BF16) for i in range(2)]
    bdA = [S.sb(f"bdA{h}", [128, NCH * 128], BF16) for h in range(4)]
    bdT = [S.sb(f"bdT{h}", [128, 3 * NCH * 128], BF16) for h in range(4)]
    rhat = [S.sb(f"rhat{h}", [128, NT], BF16) for h in range(4)]
    ABon = [S.sb(f"ABon{h}", [128, NCH * 256], BF16) for h in range(4)]
    Tt = [S.sb(f"Tt{h}", [128, NCH * 128], BF16) for h in range(4)]
    bonus = [S.sb(f"bonus{h}", [128, NT], BF16) for h in range(4)]
    gT = [S.sb(f"gT{h}", [128, NT], BF16) for h in range(4)]
    pcs = S.sb("pcs", [128, 4 * NCH], F32)
    RHSb = S.sb("RHSb", [128, 4 * 128], BF16)
    Ub = S.sb("Ub", [128, 4 * 128], BF16)

    for t_ in bdA + bdBK[0] + bdBK[1]:
        memset("pool", t_[:], 0.0)
    for c in range(NCH):
        memset("pool", ePs[:, c * 65:c * 65 + 1], 1.0)

    def layernorm_stats(src, width, key):
        stat, mv, rstd, nmr = lnscr[key]
        nchunk = width // 512
        for i in range(nchunk):
            a = src[:, i * 512:(i + 1) * 512]
            o = stat[:, i * 6:(i + 1) * 6]
            S.op("dve", lambda e, a=a, o=o: e.bn_stats(o.ap, a.ap), [a], [o])
        si = stat[:, 0:6 * nchunk]
        S.op("dve", lambda e: e.bn_aggr(mv[:].ap, si.ap), [si], [mv[:]])
        act(rstd[:], mv[:, 1:2], AF.Sqrt, bias=cst[:, 0:1], scale=1.0)
        recip(rstd[:], rstd[:])
        stt(nmr[:], mv[:, 0:1], -1.0, rstd[:], ALU.mult, ALU.mult)
        return rstd, nmr

    def proj_fm(col0, M, hT):
        bi = nb()
        for kc in range(8):
            mm(pv(bi, 0, NT, 0, M), w_in[:, kc * IN_COLS + col0:kc * IN_COLS + col0 + M], hT[kc][:], start=(kc == 0), stop=(kc == 7))
        return bi

    def shift_mix(bi, ci, out, p0=0, p1=128):
        cr, cc = (carry, ci) if ci < 12 else (carryL, ci - 12)
        ps = pv(bi, 0, NT, p0, p1)
        S.op("pool", lambda e: e.tensor_copy(Bmu[p0:p1, 0:1].ap, cr[p0:p1, cc:cc + 1].ap), [cr[:]], [Bmu[:]])
        act(Bmu[p0:p1, 1:NT + 1], ps, AF.Identity, scale=pcol[p0:p1, PC_MU + ci:PC_MU + ci + 1])
        stt(out, ps, omm[p0:p1, ci:ci + 1], Bmu[p0:p1, 0:NT], ALU.mult, ALU.add)
        S.op("pool", lambda e: e.tensor_copy(cr[p0:p1, cc:cc + 1].ap, Bmu[p0:p1, NT:NT + 1].ap), [Bmu[:]], [cr[:]])

    def halves(v):
        return [v[h * 64:(h + 1) * 64, :].re("p (c t) -> p c t", t=64) for h in range(2)]

    def bdhalves(tile):
        return [tile[h * 64:(h + 1) * 64, :].re("p (c t) -> p c t", t=128)[:, :, h * 64:(h + 1) * 64] for h in range(2)]

    def stage_x(b, tok0, par_):
        xt, hT = xt2[par_], hT2[par_]
        for j in range(NJ):
            S.dma("sp", xt[j][:], X[tok0 + j * 128:tok0 + (j + 1) * 128, :])
        for j in range(NJ):
            rstd, nmr = layernorm_stats(xt[j][:], D, "x")
            act(xt[j][:], xt[j][:], AF.Identity, bias=nmr[:], scale=rstd[:])
            yield
            tt("dve", xt[j][:], xt[j][:], embg, ALU.mult)
            tt("dve", xt[j][:], xt[j][:], embb, ALU.add)
            cp("act", x0b[:], xt[j][:])
            yield
            bi = nb()
            for c in range(8):
                tr(pvb(bi, c * 128, (c + 1) * 128), x0b[:, c * 128:(c + 1) * 128])
            for c in range(8):
                act(hT[c][:, j * 128:(j + 1) * 128], pvb(bi, c * 128, (c + 1) * 128), AF.Identity, bias=modcol(0, c, b), scale=modcol(1, c, b))
            yield

    def stage_lora(par_, first):
        hT = hT2[par_]
        tw, sgt, sgt2 = tw_p[par_], sgt_p[par_], sgt2_p[par_]
        if first:
            memset("pool", carryL[:], 0.0)
        bi = proj_fm(2560, 128, hT)
        shift_mix(bi, 12, ltmp[:], 0, 128)
        act(tw[0:64, :], ltmp[0:64, :], AF.Tanh)
        cp("pool", tw[64:128, :], ltmp[64:128, :])
        yield
        bi = proj_fm(2688, 128, hT)
        shift_mix(bi, 13, ltmp[:], 0, 128)
        act(sgt[:], ltmp[:], AF.Tanh, scale=0.5)
        yield
        bi = proj_fm(2816, 32, hT)
        shift_mix(bi, 14, ltmp[0:32, :], 0, 32)
        act(sgt2[:], ltmp[0:32, :], AF.Tanh, scale=0.5)
        yield

    def stage_gmlp_u(par_):
        hT = hT2[par_]
        for g in range(4):
            bi = proj_fm(g * 128, 128, hT)
            act(uT[g][:], pv(bi, 0, NT), AF.Gelu_apprx_tanh)
            yield

    def stage_gmlp(par_):
        hT = hT2[par_]
        for j in range(NJ):
            bi = nb()
            for kc in range(8):
                mm(pv(bi), hT[kc][:, j * 128:(j + 1) * 128], w_in[:, kc * IN_COLS + 512:kc * IN_COLS + 1024],
                   start=(kc == 0), stop=(kc == 7))
            act(gv[:], pv(bi), AF.Gelu_apprx_tanh)
            yield
            rstd, nmr = layernorm_stats(gv[:], 512, "g")
            act(gv[:], gv[:], AF.Identity, bias=nmr[:], scale=rstd[:])
            yield
            tt("dve", gv[:], gv[:], sgg, ALU.mult)
            tt("dve", vb[j][:], gv[:], sgb, ALU.add)
            yield
        for g in range(4):
            bi = nb()
            for j in range(NJ):
                o = pv(bi, j * 128, (j + 1) * 128)
                mm(o, vb[j][:, g * 128:(g + 1) * 128], WmT[:, g * 128:(g + 1) * 128], start=True, stop=False)
                mm(o, onesrow[0:1, :], bsrow[0:1, g * 128:(g + 1) * 128], start=False, stop=True)
            tt("dve", mixT[g][:], pv(bi, 0, NT), uT[g][:], ALU.mult)
            yield

    def prep_R(hp, par_, first):
        hT = hT2[par_]
        tw, sgt, sgt2 = tw_p[par_], sgt_p[par_], sgt2_p[par_]
        hs = slice(hp * 128, (hp + 1) * 128)
        bi = proj_fm(1024 + hp * 128, 128, hT)
        shift_mix(bi, hp, rS[:])
        yield
        bi = nb()
        mm(pv(bi, 0, NT), WA[0:64, hs], tw[0:64, :])
        act(th[:], pv(bi, 0, NT), AF.Tanh, bias=hw0[:, hp:hp + 1], scale=0.5)
        ts("dve", th[:], th[:], 0.5, ALU.mult, 0.5, ALU.add)
        yield
        S.op("dve", lambda e: e.tensor_tensor_scan(cs_[:].ap, resetm.ap, th[:].ap, 0.0, ALU.mult, ALU.add),
             [resetm, th[:]], [cs_[:]])
        cs3 = cs_[:].re("p (c t) -> p c t", t=64)
        act(eP3[:, :, 1:65], cs3, AF.Exp, scale=-C0)
        act(eiP[:], cs_[:], AF.Exp, scale=C0)
        cp("pool", pcs[:, hp * NCH:(hp + 1) * NCH].re("p (c o) -> p c o", o=1), eP3[:, :, 64:65])
        yield
        bi = proj_fm(2048 + hp * 128, 128, hT)
        shift_mix(bi, 8 + hp, vS[:])
        yield
        bi = nb()
        mm(pv(bi, 0, NT), GU[:, hs], sgt[:], start=True, stop=False)
        mm(pv(bi, 0, NT), GU2[0:32, hs], sgt2[0:32, :], start=False, stop=False)
        mm(pv(bi, 0, NT), GU[:, hs], onesb[:], start=False, stop=False)
        mm(pv(bi, 0, NT), GU2[0:32, hs], onesb[0:32, :], start=False, stop=True)
        act(gT[hp][:], pv(bi, 0, NT), AF.Identity, scale=0.5)
        yield
        tt("dve", rhat[hp][:].re("p (c t) -> p c t", t=64), rS[:].re("p (c t) -> p c t", t=64), eP3[:, :, 1:65], ALU.mult)
        oV = bdhalves(bdBK[hp % 2][2])
        vSh = halves(vS[:])
        for h in range(2):
            cp("act", oV[h], vSh[h])
        yield

    def prep_K(hp, par_, first):
        hT = hT2[par_]
        tw = tw_p[par_]
        hs = slice(hp * 128, (hp + 1) * 128)
        if first and hp == 0:
            memset("pool", carry[:], 0.0)
        bi = proj_fm(1536 + hp * 128, 128, hT)
        shift_mix(bi, 4 + hp, kS[:])
        yield
        bi = nb()
        mm(pv(bi, 0, NT), WA[64:128, hs], tw[64:128, :])
        act(tha[:], pv(bi, 0, NT), AF.Tanh, bias=ha0[:, hp:hp + 1], scale=0.5)
        yield
        ts("dve", kk0[:], kS[:], pcol[:, PC_KK + hp:PC_KK + hp + 1], ALU.mult)
        act(sq[:], kS[:], AF.Square, scale=pcol[:, PC_KK + hp:PC_KK + hp + 1])
        yield
        bi = nb()
        mm(pv(bi, 0, NT), bones, sq[:])
        ts("dve", sq[:], pv(bi, 0, NT), 1e-24, ALU.max)
        yield
        recip(sq[:], sq[:])
        tt("dve", kkn[:], kk0[:], sq[:], ALU.mult)
        yield
        ts("dve", f1[:], tha[:], hka[:, hp:hp + 1], ALU.mult, omhka[:, hp:hp + 1], ALU.add)
        tt("dve", km[:], kS[:], f1[:], ALU.mult)
        stt(t1[:], tha[:], 1.0, kk0[:], ALU.add, ALU.mult)
        yield

    def prep_J(hp):
        bdB, bdK, bdV = bdBK[hp % 2]
        kknh, t1h, eiPh, kmh = halves(kkn[:]), halves(t1[:]), halves(eiP[:]), halves(km[:])
        oA, oB, oK = bdhalves(bdA[hp]), bdhalves(bdB), bdhalves(bdK)
        for h in range(2):
            ePm_h = eP3[h * 64:(h + 1) * 64, :, 0:64]
            stt(oA[h], kknh[h], -0.5, ePm_h, ALU.mult, ALU.mult)
            tt("dve", oB[h], t1h[h], eiPh[h], ALU.mult)
            tt("dve", oK[h], kmh[h], eiPh[h], ALU.mult)
            yield
        stt(rk[:], rS[:], pcol[:, PC_RK + hp:PC_RK + hp + 1], km[:], ALU.mult, ALU.mult)
        bi = nb()
        mm(pv(bi, 0, NT), bones, rk[:])
        tt("dve", bonus[hp][:], pv(bi, 0, NT), vS[:], ALU.mult)
        yield
        bp = nb2()
        for n_, src in enumerate((bdB, bdK, bdV)):
            for c in range(NCH):
                k0 = (n_ * NCH + c) * 128
                tr(pvmb(bp, 2, k0, k0 + 128), src[:, c * 128:(c + 1) * 128])
        cp("act", bdT[hp][:], pvmb(bp, 2, 0, 3 * NCH * 128))
        yield

    def stage_prep(hp, par_, first):
        return seq(par(prep_K(hp, par_, first), prep_R(hp, par_, first)), prep_J(hp))

    def stage_T(hp, cg):
        bdB, bdK, bdV = bdBK[hp % 2]
        Lt, ABo = LtG[cg], ABoffG[cg]
        sl = lambda t_, c: t_[:, c * 128:(c + 1) * 128]
        B0 = G1 + cg * NCG
        cbufs = tuple(bbuf[B0:B0 + NCG])
        g3 = lambda c0, c1: V(PS.h[:, B0 * 512:(B0 + NCG) * 512].rearrange("p (c x) -> p c x", x=512)[:, :, c0:c1], cbufs)
        for i in range(NCG):
            c = cg * NCG + i
            rh = rhat[hp][:, c * 64:(c + 1) * 64]
            mm(pv(B0 + i, 0, 128), sl(bdB, c), sl(bdA[hp], c))
            mm(pv(B0 + i, 128, 256), sl(bdA[hp], c), sl(bdB, c))
            mm(pv(B0 + i, 256, 384), sl(bdK, c), sl(bdA[hp], c))
            mm(pv(B0 + i, 384, 448), sl(bdB, c), rh)
            mm(pv(B0 + i, 448, 512), sl(bdK, c), rh)
        abo3 = ABo[:].re("p (c x) -> p c x", x=256)
        abn3 = ABon[hp][:, cg * NCG * 256:(cg + 1) * NCG * 256].re("p (c x) -> p c x", x=256)
        mk3a = V(consts.h[:, CO_MASK:CO_MASK + 256].unsqueeze(1).to_broadcast([128, NCG, 256]), (consts.buf,))
        mk3b = V(consts.h[:, CO_MASK + 256:CO_MASK + 512].unsqueeze(1).to_broadcast([128, NCG, 256]), (consts.buf,))
        tt("dve", abo3, g3(0, 256), mk3a, ALU.mult)
        tt("dve", abn3, g3(256, 512), mk3b, ALU.mult)
        yield
        l3 = Lt[:].re("p (c x) -> p c x", x=512)
        id3 = V(identb.h[:, 0:128].unsqueeze(1).to_broadcast([128, NCG, 128]), (identb.buf,))
        tt("dve", l3[:, :, 0:128], abo3[:, :, 0:128], id3, ALU.add)
        for i in range(NCG):
            M_, MT_ = ABo[:, i * 256:i * 256 + 128], ABo[:, i * 256 + 128:i * 256 + 256]
            mm(pv(B0 + i, 128, 256), MT_, M_)
            mm(pv(B0 + i, 256, 384), M_, MT_)
        cp("act", l3[:, :, 128:384], g3(128, 384))
        yield
        for k in range(1, 5):
            for i in range(NCG):
                o = i * 512
                Q, P, PT = Lt[:, o:o + 128], Lt[:, o + 128:o + 256], Lt[:, o + 256:o + 384]
                mm(pv(B0 + i, 0, 128), PT, Q)
                mm(pv(B0 + i, 128, 256), PT, P)
                mm(pv(B0 + i, 256, 384), P, PT)
            tt("dve", l3[:, :, 0:128], g3(0, 128), l3[:, :, 0:128], ALU.add)
            cp("act", l3[:, :, 128:384], g3(128, 384))
            yield
        for i in range(NCG):
            o = i * 512
            mm(pv(B0, i * 128, (i + 1) * 128), Lt[:, o + 256:o + 384], Lt[:, o:o + 128])
        tt("dve", Tt[hp][:, cg * NCG * 128:(cg + 1) * NCG * 128].re("p (c x) -> p c x", x=128),
           pv(B0, 0, NCG * 128).re("p (c x) -> p c x", x=128), l3[:, :, 0:128], ALU.add)
        yield

    def stage_onpath():
        sl = lambda t_, c: t_[:, c * 128:(c + 1) * 128]
        hsl = lambda t_, hp: t_[:, hp * 128:(hp + 1) * 128]
        for c in range(NCH):
            bi = nb()
            for hp in range(4):
                o = pv(bi, hp * 128, (hp + 1) * 128)
                mm(o, sl(bdA[hp], c), hsl(S0b, hp), start=True, stop=False)
                mm(o, ABon[hp][:, c * 256:c * 256 + 128], bdT[hp][:, (2 * NCH + c) * 128:(2 * NCH + c + 1) * 128], start=False, stop=True)
            cp("act", RHSb[:], pv(bi))
            yield
            bi = nb()
            for hp in range(4):
                mm(pv(bi, hp * 128, (hp + 1) * 128), sl(Tt[hp], c), hsl(RHSb, hp))
            cp("dve", Ub[:], pv(bi))
            yield
            bi = nb()
            for hp in range(4):
                o = pv(bi, hp * 128, (hp + 1) * 128)
                Bt_c = bdT[hp][:, c * 128:(c + 1) * 128]
                Kt_c = bdT[hp][:, (NCH + c) * 128:(NCH + c + 1) * 128]
                Vt_c = bdT[hp][:, (2 * NCH + c) * 128:(2 * NCH + c + 1) * 128]
                mm(o, ident32, hsl(S32, hp), start=True, stop=False)
                mm(o, Bt_c, hsl(Ub, hp), start=False, stop=False)
                mm(o, Kt_c, Vt_c, start=False, stop=True)
            for hp in range(4):
                Vt_c = bdT[hp][:, (2 * NCH + c) * 128:(2 * NCH + c + 1) * 128]
                rh = rhat[hp][:, c * 64:(c + 1) * 64]
                yo = pv(G1 + hp, c * 64, (c + 1) * 64)
                mm(yo, hsl(S0b, hp), rh, start=True, stop=False)
                mm(yo, hsl(Ub, hp), ABon[hp][:, c * 256 + 128:c * 256 + 192], start=False, stop=False)
                mm(yo, Vt_c, ABon[hp][:, c * 256 + 192:c * 256 + 256], start=False, stop=True)
            pc3 = V(pcs.h[:, :].rearrange("p (h c) -> p h c", c=NCH)[:, :, c:c + 1].to_broadcast([128, 4, 128]), (pcs.buf,))
            s3 = lambda t_: t_[:].re("p (h x) -> p h x", x=128)
            tt("dve", s3(S0b), pv(bi).re("p (h x) -> p h x", x=128), pc3, ALU.mult)
            tt("dve", s3(S32), pv(bi).re("p (h x) -> p h x", x=128), pc3, ALU.mult)
            yield

    gn_tmp = [(rS, kS, vS), (th, tha, cs_), (eiP, sq, kk0), (kkn, f1, km)]

    def stage_gn(hp):
        Y, Ysq, Z = gn_tmp[hp]
        yb = pv(G1 + hp, 0, NT)
        cp("act", Y[:], yb)
        act(Ysq[:], yb, AF.Square)
        yield
        b1 = nb()
        mm(pv(b1, 0, NT), bones, Y[:])
        b2 = nb()
        mm(pv(b2, 0, NT), bones, Ysq[:])
        stt(Z[:], pv(b1, 0, NT), -1.0 / 64, Y[:], ALU.mult, ALU.add)
        act(Y[:], pv(b1, 0, NT), AF.Square, scale=1.0 / 64)
        stt(Ysq[:], pv(b2, 0, NT), 1.0 / 64, Y[:], ALU.mult, ALU.subtract)
        yield
        act(Ysq[:], Ysq[:], AF.Sqrt, bias=cst[:, 1:2], scale=1.0)
        recip(Ysq[:], Ysq[:])
        yield
        tt("dve", Z[:], Z[:], Ysq[:], ALU.mult)
        act(Z[:], Z[:], AF.Identity, bias=pcol[:, PC_LB + hp:PC_LB + hp + 1], scale=pcol[:, PC_LG + hp:PC_LG + hp + 1])
        yield
        tt("dve", Z[:], Z[:], bonus[hp][:], ALU.add)
        tt("dve", mixT[4 + hp][:], Z[:], gT[hp][:], ALU.mult)
        yield

    def stage_out(b, tok0, par_, first):
        xt = xt2[par_]
        if first:
            S.dma("sp", gt1t[:], GT1S[b * 128:(b + 1) * 128, :])
        for j in range(NJ):
            for half in range(2):
                bi = nb()
                for kc in range(8):
                    mm(pv(bi), mixT[kc][:, j * 128:(j + 1) * 128], w_out[:, kc * D + half * 512:kc * D + (half + 1) * 512],
                       start=(kc == 0), stop=(kc == 7))
                tt("dve", r1[:, half * 512:(half + 1) * 512], pv(bi), gt1t[:, half * 512:(half + 1) * 512], ALU.mult)
                yield
            stt(r1[:], xt[j][:], ALPHA, r1[:], ALU.mult, ALU.add)
            rstd, nmr = layernorm_stats(r1[:], D, "o")
            act(r1[:], r1[:], AF.Identity, bias=nmr[:], scale=rstd[:])
            yield
            tt("dve", r1[:], r1[:], ln1g, ALU.mult)
            tt("dve", r1[:], r1[:], ln1b, ALU.add)
            blk = (tok0 + j * 128) // 128
            S.dma("sp", X1SB[blk][tok0 + j * 128:tok0 + (j + 1) * 128, :], r1[:])
            yield

    steps = [(b, s) for b in range(NSEQ) for s in range(NSTEP)]

    def stage_B(n):
        b, s = steps[n]
        return stage_x(b, b * SEQ + s * NT, n % 2)

    def stage_L(n):
        b, s = steps[n]
        return stage_lora(n % 2, s == 0)

    def reset_state():
        memset("pool", S32[:], 0.0)
        memset("pool", S0b[:], 0.0)
        yield

    TG = lambda name, g: tagged(S, name, g)
    drive(TG("0.B", stage_B(0)))
    drive(TG("0.lora", stage_L(0)))
    drive(TG("0.prep0", stage_prep(0, 0, True)))
    for n, (b, s) in enumerate(steps):
        tok0 = b * SEQ + s * NT
        pn = n % 2
        sid = f"{n}."
        if s == 0:
            drive(reset_state())
        for hp in range(4):
            gens = [TG(sid + f"T{hp}a", stage_T(hp, 0)), TG(sid + f"T{hp}b", stage_T(hp, 1))]
            if hp < 3:
                gens.insert(0, TG(sid + f"prep{hp + 1}", stage_prep(hp + 1, pn, s == 0)))
            else:
                gens.append(TG(sid + "gmlp", stage_gmlp(pn)))
            if hp == 2:
                gens.append(TG(sid + "gmlpu", stage_gmlp_u(pn)))
            drive(par(*gens))
        gens = [seq(TG(sid + "onpath", stage_onpath()), par(*[TG(sid + f"gn{hp}", stage_gn(hp)) for hp in range(4)]))]
        if n + 1 < len(steps):
            gens.append(seq(TG(f"{n + 1}.B", stage_B(n + 1)), TG(f"{n + 1}.lora", stage_L(n + 1))))
        drive(par(*gens))
        gens = [TG(sid + "out", stage_out(b, tok0, pn, s == 0))]
        if n + 1 < len(steps):
            gens.append(TG(f"{n + 1}.prep0", stage_prep(0, (n + 1) % 2, steps[n + 1][1] == 0)))
        drive(par(*gens))

    S.barrier()
    S.reset(mk_w)
    NBLKF = FFN // 256
    w_fg = [S.sb(f"w_fg{i}", [128, 8 * 256], BF16) for i in range(NBLKF)]
    w_fu = [S.sb(f"w_fu{i}", [128, 8 * 256], BF16) for i in range(NBLKF)]
    w_fo = [S.sb(f"w_fo{k}", [128, D], BF16) for k in range(NKF)]
    wfi3 = WFI.h[:, :].rearrange("(k p) n -> p k n", p=128)
    for i in range(NBLKF):
        S.dma("pool", w_fg[i][:].re("p (k n) -> p k n", n=256), V(wfi3[:, :, i * 256:(i + 1) * 256], (WFI.buf,)))
        S.dma("pool", w_fu[i][:].re("p (k n) -> p k n", n=256), V(wfi3[:, :, FFN + i * 256:FFN + (i + 1) * 256], (WFI.buf,)))
    for kc in range(NKF):
        S.dma("pool", w_fo[kc][:], WFO[kc * 128:(kc + 1) * 128, :])
    rows2 = S.sb("rows2", [128, 2 * D], F32)
    S.dma("sp", rows2[:], V(ROWS.h[0:1, 4 * D:6 * D].to_broadcast([128, 2 * D]), (ROWS.buf,)))
    ln2g, ln2b = rows2[:, 0:D], rows2[:, D:2 * D]
    gt2t = S.sb("gt2t", [128, D], F32)
    NT2 = 512
    NJ2 = NT2 // 128
    xs = S.sb("xs", [128, D], F32)
    xr = S.sb("xr", [128, D], F32)
    x1b = S.sb("x1b", [128, D], BF16)
    h2 = [S.sb(f"h2_{c}", [128, NT2], BF16) for c in range(8)]
    aT = [S.sb(f"aT{c}", [128, NT2], BF16) for c in range(NKF)]
    sgl = S.sb("sgl", [128, NT2], BF16)
    r2 = S.sb("r2", [128, D], F32)
    stat2 = S.sb("stat2", [128, 12], F32)
    mv2 = S.sb("mv2", [128, 2], F32)
    rstd2 = S.sb("rstd2", [128, 1], F32)
    nmr2 = S.sb("nmr2", [128, 1], F32)
    rr8 = [0]

    def nb8():
        i = rr8[0]
        rr8[0] = (rr8[0] + 1) % 8
        return i

    steps2 = [(b, s) for b in range(NSEQ) for s in range(SEQ // NT2)]

    def ffn_A(n):
        b, s = steps2[n]
        tok0 = b * SEQ + s * NT2
        for j in range(NJ2):
            blk = (tok0 + j * 128) // 128
            S.dma("sp", xs[:], X1SB[blk][tok0 + j * 128:tok0 + (j + 1) * 128, :])
            cp("act", x1b[:], xs[:])
            yield
            bi = nb8()
            for c in range(8):
                tr(pvb(bi, c * 128, (c + 1) * 128), x1b[:, c * 128:(c + 1) * 128])
            for c in range(8):
                act(h2[c][:, j * 128:(j + 1) * 128], pvb(bi, c * 128, (c + 1) * 128), AF.Identity, bias=modcol(2, c, b), scale=modcol(3, c, b))
            yield

    def ffn_B(n):
        for m_ in range(NKF):
            bg = nb8()
            for kc in range(8):
                mm(pv(bg), w_fg[m_ // 2][:, kc * 256 + (m_ % 2) * 128:kc * 256 + (m_ % 2) * 128 + 128], h2[kc][:], start=(kc == 0), stop=(kc == 7))
            bu = nb8()
            for kc in range(8):
                mm(pv(bu), w_fu[m_ // 2][:, kc * 256 + (m_ % 2) * 128:kc * 256 + (m_ % 2) * 128 + 128], h2[kc][:], start=(kc == 0), stop=(kc == 7))
            act(sgl[:], pv(bg), AF.Silu)
            tt("dve", aT[m_][:], pv(bu), sgl[:], ALU.mult)
            yield

    def ffn_C(n):
        b, s = steps2[n]
        tok0 = b * SEQ + s * NT2
        if s == 0:
            S.dma("sp", gt2t[:], GT2S[b * 128:(b + 1) * 128, :])
        for j in range(NJ2):
            blk = (tok0 + j * 128) // 128
            S.dma("sp", xr[:], X1SB[blk][tok0 + j * 128:tok0 + (j + 1) * 128, :])
            for half in range(2):
                bi = nb8()
                for kc in range(NKF):
                    mm(pv(bi), aT[kc][:, j * 128:(j + 1) * 128], w_fo[kc][:, half * 512:(half + 1) * 512],
                       start=(kc == 0), stop=(kc == NKF - 1))
                tt("dve", r2[:, half * 512:(half + 1) * 512], pv(bi), gt2t[:, half * 512:(half + 1) * 512], ALU.mult)
                yield
            stt(r2[:], xr[:], ALPHA, r2[:], ALU.mult, ALU.add)
            for i in range(2):
                a = r2[:, i * 512:(i + 1) * 512]
                o = stat2[:, i * 6:(i + 1) * 6]
                S.op("dve", lambda e, a=a, o=o: e.bn_stats(o.ap, a.ap), [a], [o])
            S.op("dve", lambda e: e.bn_aggr(mv2[:].ap, stat2[:].ap), [stat2[:]], [mv2[:]])
            act(rstd2[:], mv2[:, 1:2], AF.Sqrt, bias=cst[:, 0:1], scale=1.0)
            recip(rstd2[:], rstd2[:])
            stt(nmr2[:], mv2[:, 0:1], -1.0, rstd2[:], ALU.mult, ALU.mult)
            yield
            act(r2[:], r2[:], AF.Identity, bias=nmr2[:], scale=rstd2[:])
            tt("dve", r2[:], r2[:], ln2g, ALU.mult)
            tt("dve", r2[:], r2[:], ln2b, ALU.add)
            S.dma("sp", OUTB[blk][tok0 + j * 128:tok0 + (j + 1) * 128, :], r2[:])
            yield

    drive(ffn_A(0))
    for n in range(len(steps2)):
        drive(ffn_B(n))
        gens = [ffn_C(n)]
        if n + 1 < len(steps2):
            gens.append(ffn_A(n + 1))
        drive(par(*gens))

    S.emit_all()
    return nc, S


def _consts():
    c = np.zeros((128, CO_N), np.float32)
    p = np.arange(128)
    hb = p // 64
    pi = p % 64
    col = np.arange(128)
    ch, ci = col // 64, col % 64
    same = hb[:, None] == ch[None, :]
    up = same & (pi[:, None] < ci[None, :])
    lo = same & (pi[:, None] > ci[None, :])
    c[:, 0:128] = up
    c[:, 128:256] = lo
    c[:, 256:384] = up
    t = np.arange(64)
    le = (pi[:, None] <= t[None, :])
    c[:, 384:448] = le
    c[:, 448:512] = le
    r = np.ones((128, 256), np.float32)
    r[:, ::64] = 0.0
    c[:, CO_RESET:CO_RESET + 256] = r
    c[:, CO_ID:CO_ID + 128] = np.eye(128)
    c[:, CO_ONES:CO_ONES + 128] = same
    c[:, CO_TRIU:CO_TRIU + 128] = (p[:, None] <= col[None, :])
    return c


def _colchunks(v, n):
    return np.ascontiguousarray(v.reshape(n, 128).T)


_CACHE = {}


def kernel(x, c, emb_ln_g, emb_ln_b, w_ada, b_ada, w_in, mu_shift, sg_ln_g, sg_ln_b, w_s, b_s,
           w0, w_up, a0, a_up, g_up, k_k, k_a, r_k, lnx_g, lnx_b, w_out, ln1_g, ln1_b,
           w_ffn_in, w_ffn_out, ln2_g, ln2_b):
    f = lambda a: np.ascontiguousarray(np.asarray(a, dtype=np.float32))
    x, c = f(x), f(c)
    if "nc" not in _CACHE:
        _CACHE["nc"] = build_program()
    nc, S = _CACHE["nc"]
    mu = f(mu_shift)[0]
    pc = np.zeros((128, PC_N), np.float32)
    pc[:, 0:12] = _colchunks(mu[0:1536], 12)
    pc[:, 12] = mu[1536:1664]
    pc[:, 13] = mu[1664:1792]
    pc[0:32, 14] = mu[1792:1824]
    pc[:, PC_W0:PC_W0 + 4] = _colchunks(f(w0)[0], 4)
    pc[:, PC_A0:PC_A0 + 4] = _colchunks(f(a0)[0], 4)
    pc[:, PC_KK:PC_KK + 4] = _colchunks(f(k_k)[0], 4)
    pc[:, PC_KA:PC_KA + 4] = _colchunks(f(k_a)[0], 4)
    pc[:, PC_RK:PC_RK + 4] = _colchunks(f(r_k)[0].reshape(-1), 4)
    pc[:, PC_LG:PC_LG + 4] = _colchunks(f(lnx_g)[0], 4)
    pc[:, PC_LB:PC_LB + 4] = _colchunks(f(lnx_b)[0], 4)
    pc[:, PC_BADA:PC_BADA + 48] = _colchunks(f(b_ada)[0], 48)
    rows = np.concatenate([f(emb_ln_g), f(emb_ln_b), f(ln1_g)[0], f(ln1_b)[0], f(ln2_g)[0], f(ln2_b)[0],
                           f(sg_ln_g)[0], f(sg_ln_b)[0]])[None, :]
    w_sT = np.ascontiguousarray(np.transpose(f(w_s)[0], (2, 0, 1)).reshape(128, 512))
    w_ua = np.ascontiguousarray(np.concatenate([f(w_up)[0], f(a_up)[0]], axis=0))
    common = {
        "consts": _consts(), "w_ada": f(w_ada)[0], "b_ada": f(b_ada), "w_in": f(w_in)[0], "w_out": f(w_out)[0],
        "w_ffn_in": f(w_ffn_in)[0], "w_ffn_out": f(w_ffn_out)[0], "rows": np.ascontiguousarray(rows),
        "b_s": f(b_s)[0].reshape(1, 512), "w_sT": w_sT, "w_ua": w_ua, "g_up": f(g_up)[0],
    }
    in_maps = []
    for i in range(8):
        pci = pc.copy()
        for b in range(NSEQ):
            pci[:, PC_CT + b * 8:PC_CT + (b + 1) * 8] = _colchunks(c[i * NSEQ + b], 8)
        m = dict(common)
        m["pcol"] = pci
        m["x"] = np.ascontiguousarray(x[i * NSEQ:(i + 1) * NSEQ].reshape(NSEQ * SEQ, D))
        in_maps.append(m)
    res = run_bass_kernel_spmd(nc, in_maps, core_ids=list(range(8)))
    out = np.concatenate([r["out"].reshape(NSEQ, SEQ, D) for r in res.results], axis=0)
    return out.astype(np.float32)
```

```python
import numpy as np
import ml_dtypes
import concourse.bass as bass
import concourse.mybir as mybir
from concourse.bass_utils import run_bass_kernel_spmd

F32 = mybir.dt.float32
BF16 = mybir.dt.bfloat16
AF = mybir.ActivationFunctionType
ALU = mybir.AluOpType
AX = mybir.AxisListType

SAME_ENGINE_SYNC = True
EMBED_WAIT = True
N_HW_CH = 8
N_SW_CH = 4
N_DMA_CH = N_HW_CH + N_SW_CH
SB_BASE = 16512
SB_END = 229376

D = 1024
SEQ = 2048
NSEQ = 2
NT = 256
NJ = NT // 128
NCH = NT // 64
NSTEP = SEQ // NT
IN_COLS = 2848
FFN = 2816
NKF = FFN // 128
ALPHA = 2.0 ** 0.25
C0 = float(np.exp(-0.5))
LN_EPS = 1e-5
GN_EPS = 64e-5

PC_MU, PC_W0, PC_A0, PC_KK, PC_KA, PC_RK, PC_LG, PC_LB, PC_BADA, PC_CT, PC_N = 0, 19, 23, 27, 31, 35, 39, 43, 47, 95, 111
CO_MASK, CO_RESET, CO_ID, CO_ONES, CO_TRIU, CO_N = 0, 512, 768, 896, 1024, 1152


class Buf:
    __slots__ = ("name", "last_w", "readers")

    def __init__(self, name):
        self.name = name
        self.last_w = None
        self.readers = []


class Op:
    __slots__ = ("eng", "emit", "deps", "signal", "pos", "sigval", "is_dma", "ch", "chval", "tag", "know", "gid")

    def __init__(self, eng, emit, is_dma=False):
        self.eng = eng
        self.emit = emit
        self.deps = []
        self.signal = False
        self.pos = -1
        self.sigval = -1
        self.is_dma = is_dma
        self.ch = None
        self.chval = 0
        self.tag = None
        self.know = None
        self.gid = 0


class T:
    def __init__(self, h, buf):
        self.h = h
        self.buf = buf

    def __getitem__(self, key):
        return V(self.h[key], (self.buf,))


class TA:
    def __init__(self, h, bufs):
        self.h = h
        self.bufs = bufs

    def __getitem__(self, key):
        return V(self.h[key], self.bufs)


class V:
    __slots__ = ("ap", "bufs")

    def __init__(self, ap, bufs):
        self.ap = ap
        self.bufs = bufs

    def __getitem__(self, key):
        return V(self.ap[key], self.bufs)

    def re(self, s, **kw):
        return V(self.ap.rearrange(s, **kw), self.bufs)

    def bc(self, shape):
        return V(self.ap.to_broadcast(shape), self.bufs)


class Sched:
    ENGS = ("pe", "act", "dve", "pool", "sp")

    def __init__(self, nc):
        self.nc = nc
        self.ops = {e: [] for e in self.ENGS}
        self.seen = {e: {} for e in self.ENGS}
        self.ch_last = [None] * N_DMA_CH
        self.ch_cnt = [0] * N_DMA_CH
        self.ch_rr = {"hw": 0, "sw": 0}
        self.n_t = 0
        self.off = SB_BASE
        self.tag = ""
        self.gid = 0

    def mark(self):
        return self.off

    def reset(self, off):
        self.off = off

    def sb(self, name, shape, dtype, buf=None):
        self.n_t += 1
        nm = f"{name}_{self.n_t}"
        esz = 4 if dtype == F32 else 2
        n = 1
        for s in shape[1:]:
            n *= s
        nbytes = (n * esz + 63) // 64 * 64
        off = self.off
        self.off += nbytes
        assert self.off <= SB_END, f"SBUF overflow at {name}: {self.off}"
        h = self.nc.alloc_sbuf_tensor_at(nm, list(shape), dtype, offset=off)
        t = T(h, buf if buf is not None else Buf(nm))
        t.off = off
        return t

    def alias(self, name, shape, dtype, over):
        self.n_t += 1
        nm = f"{name}_{self.n_t}"
        h = self.nc.alloc_sbuf_tensor_at(nm, list(shape), dtype, offset=over[0].off)
        t = TA(h, tuple(o.buf for o in over))
        t.off = over[0].off
        return t

    def ps(self, name, shape, dtype=F32):
        self.n_t += 1
        nm = f"{name}_{self.n_t}"
        h = self.nc.alloc_psum_tensor(nm, list(shape), dtype)
        return T(h, Buf(nm))

    def _add(self, op, reads, writes):
        E = op.eng
        deps = []
        raw = set()
        for v in reads:
            for b in v.bufs:
                if b.last_w is not None:
                    deps.append(b.last_w)
                    raw.add(id(b.last_w))
        for v in writes:
            for b in v.bufs:
                if b.last_w is not None:
                    deps.append(b.last_w)
                deps.extend(b.readers)
        seen = self.seen[E]
        need = {}
        for d in deps:
            if d is op:
                continue
            if d.is_dma:
                key = ("ch", d.ch)
                if seen.get(key, 0) >= d.chval:
                    continue
                if key not in need or need[key].chval < d.chval:
                    need[key] = d
            else:
                if d.eng == E and (E == "pe" or not SAME_ENGINE_SYNC or id(d) not in raw):
                    continue
                key = ("e", d.eng)
                if seen.get(key, -1) >= d.pos:
                    continue
                if key not in need or need[key].pos < d.pos:
                    need[key] = d
        for key, d in sorted(need.items(), key=lambda kv: -kv[1].gid):
            val = d.chval if d.is_dma else d.pos
            if seen.get(key, -1) >= val:
                continue
            seen[key] = val
            if not d.is_dma:
                d.signal = True
                for k2, v2 in d.know.items():
                    if seen.get(k2, -1) < v2:
                        seen[k2] = v2
            op.deps.append(d)
        op.pos = len(self.ops[E])
        op.tag = self.tag
        self.gid += 1
        op.gid = self.gid
        if not op.is_dma:
            op.know = dict(seen)
            op.know[("e", E)] = op.pos
        self.ops[E].append(op)
        for v in reads:
            for b in v.bufs:
                b.readers.append(op)
        for v in writes:
            for b in v.bufs:
                b.last_w = op
                b.readers = []
        return op

    def op(self, eng, emit, reads=(), writes=()):
        return self._add(Op(eng, emit), list(reads), list(writes))

    def dma(self, q, out, in_, **kw):
        op = Op(q, lambda e: e.dma_start(out.ap, in_.ap, **kw), is_dma=True)
        if q == "pool":
            c = N_HW_CH + self.ch_rr["sw"]
            self.ch_rr["sw"] = (self.ch_rr["sw"] + 1) % N_SW_CH
        else:
            c = self.ch_rr["hw"]
            self.ch_rr["hw"] = (self.ch_rr["hw"] + 1) % N_HW_CH
        op.ch = c
        self.ch_cnt[c] += 16
        op.chval = self.ch_cnt[c]
        prev = self.ch_last[c]
        self.ch_last[c] = op
        if prev is not None:
            key = ("ch", c)
            if self.seen[q].get(key, 0) < prev.chval:
                self.seen[q][key] = prev.chval
                op.deps.append(prev)
        return self._add(op, [in_], [out])

    def barrier(self):
        lasts = {e: (self.ops[e][-1] if self.ops[e] else None) for e in self.ENGS}
        chl = list(self.ch_last)
        for e in self.ENGS:
            op = Op(e, lambda eng: eng.nop())
            seen = self.seen[e]
            for f in self.ENGS:
                d = lasts[f]
                if f == e or d is None:
                    continue
                k = lasts[f].pos
                while k >= 0 and self.ops[f][k].is_dma:
                    k -= 1
                if k < 0:
                    continue
                d = self.ops[f][k]
                if seen.get(("e", f), -1) < d.pos:
                    seen[("e", f)] = d.pos
                    d.signal = True
                    op.deps.append(d)
            for c, d in enumerate(chl):
                if d is not None and seen.get(("ch", c), 0) < d.chval:
                    seen[("ch", c)] = d.chval
                    op.deps.append(d)
            op.pos = len(self.ops[e])
            self.ops[e].append(op)

    def emit_all(self):
        nc = self.nc
        from contextlib import ExitStack

        for e in self.ENGS:
            n = 0
            for o in self.ops[e]:
                if o.signal and not o.is_dma:
                    n += 1
                    o.sigval = n
        with ExitStack() as st:
            esem = {e: st.enter_context(nc.semaphore(f"s_{e}")) for e in self.ENGS}
            csem = [st.enter_context(nc.semaphore(f"s_ch{c}")) for c in range(N_DMA_CH)]
            block = st.enter_context(nc.Block())

            def run(e, eng):
                for o in self.ops[e]:
                    deps = o.deps
                    last = None
                    if deps and EMBED_WAIT:
                        deps, last = deps[:-1], deps[-1]
                    for d in deps:
                        if d.is_dma:
                            eng.wait_ge(csem[d.ch], d.chval)
                        else:
                            eng.wait_ge(esem[d.eng], d.sigval)
                    ins = o.emit(eng)
                    if last is not None:
                        if last.is_dma:
                            ins._wait_ge(csem[last.ch], last.chval)
                        else:
                            ins._wait_ge(esem[last.eng], last.sigval)
                    if o.is_dma:
                        ins.then_inc(csem[o.ch], 16)
                    elif o.signal:
                        ins.then_inc(esem[e], 1)
                if e == "sp":
                    for c in range(N_DMA_CH):
                        if self.ch_cnt[c]:
                            eng.wait_ge(csem[c], self.ch_cnt[c])

            @block.tensor
            def _(eng):
                run("pe", eng)

            @block.scalar
            def _(eng):
                run("act", eng)

            @block.vector
            def _(eng):
                run("dve", eng)

            @block.gpsimd
            def _(eng):
                run("pool", eng)

            @block.sync
            def _(eng):
                run("sp", eng)


def seq(*gens):
    for g in gens:
        yield from g


def par(*gens):
    gens = list(gens)
    while gens:
        for g in list(gens):
            try:
                next(g)
            except StopIteration:
                gens.remove(g)
                continue
            yield


def tagged(S, name, g):
    while True:
        S.tag = name
        try:
            next(g)
        except StopIteration:
            return
        yield


def par_w(main, fill, ratio):
    main_done = fill_done = False
    while not (main_done and fill_done):
        for _ in range(ratio):
            if main_done:
                break
            try:
                next(main)
            except StopIteration:
                main_done = True
                break
            yield
        if not fill_done:
            try:
                next(fill)
            except StopIteration:
                fill_done = True
                continue
            yield


def drive(g):
    for _ in g:
        pass


def build_program():
    nc = bass.Bass("TRN2", target_bir_lowering=False)
    S = Sched(nc)

    def din(name, shape, dt=F32):
        return T(nc.dram_tensor(name, list(shape), dt, kind="ExternalInput"), Buf(name))

    X = din("x", [NSEQ * SEQ, D])
    PCOL = din("pcol", [128, PC_N])
    CONST = din("consts", [128, CO_N])
    WADA = din("w_ada", [D, 6 * D])
    BADA = din("b_ada", [1, 6 * D])
    WIN = din("w_in", [D, IN_COLS])
    WOUT = din("w_out", [D, D])
    WFI = din("w_ffn_in", [D, 2 * FFN])
    WFO = din("w_ffn_out", [FFN, D])
    ROWS = din("rows", [1, 6 * D + 1024])
    BS = din("b_s", [1, 512])
    WST = din("w_sT", [128, 512])
    WUA = din("w_ua", [128, 512])
    GUP = din("g_up", [160, 512])
    out_h = nc.dram_tensor("out", [NSEQ * SEQ, D], F32, kind="ExternalOutput")
    x1s_h = nc.dram_tensor("x1s", [NSEQ * SEQ, D], F32, kind="Internal")
    gt2s_h = nc.dram_tensor("gt2s", [NSEQ * 128, D], F32, kind="Internal")
    gt1s_h = nc.dram_tensor("gt1s", [NSEQ * 128, D], F32, kind="Internal")
    NBLK = NSEQ * SEQ // 128
    OUTB = [T(out_h, Buf(f"out{i}")) for i in range(NBLK)]
    X1SB = [T(x1s_h, Buf(f"x1s{i}")) for i in range(NBLK)]
    GT2S = T(gt2s_h, Buf("gt2s"))
    GT1S = T(gt1s_h, Buf("gt1s"))

    def act(out, in_, func, bias=None, scale=None):
        reads = [in_]
        kw = {}
        if isinstance(bias, V):
            reads.append(bias)
            kw["bias"] = bias.ap
        elif bias is not None:
            kw["bias"] = bias
        if isinstance(scale, V):
            reads.append(scale)
            kw["scale"] = scale.ap
        elif scale is not None:
            kw["scale"] = scale
        S.op("act", lambda e: e.activation(out.ap, in_.ap, func, **kw), reads, [out])

    def tt(eng, out, a, b, op):
        S.op(eng, lambda e: e.tensor_tensor(out.ap, a.ap, b.ap, op), [a, b], [out])

    def ts(eng, out, a, s1, op0, s2=None, op1=None):
        reads = [a]
        v1 = s1.ap if isinstance(s1, V) else s1
        v2 = s2.ap if isinstance(s2, V) else s2
        if isinstance(s1, V):
            reads.append(s1)
        if isinstance(s2, V):
            reads.append(s2)
        if op1 is None:
            S.op(eng, lambda e: e.tensor_scalar(out.ap, a.ap, v1, None, op0), reads, [out])
        else:
            S.op(eng, lambda e: e.tensor_scalar(out.ap, a.ap, v1, v2, op0, op1), reads, [out])

    def stt(out, a, sc, b, op0, op1):
        reads = [a, b]
        sv = sc.ap if isinstance(sc, V) else sc
        if isinstance(sc, V):
            reads.append(sc)
        S.op("dve", lambda e: e.scalar_tensor_tensor(out.ap, a.ap, sv, b.ap, op0, op1), reads, [out])

    def cp(eng, out, a):
        if eng == "act":
            S.op("act", lambda e: e.activation(out.ap, a.ap, AF.Copy), [a], [out])
        else:
            S.op(eng, lambda e: e.tensor_copy(out.ap, a.ap), [a], [out])

    def recip(out, a):
        S.op("dve", lambda e: e.reciprocal(out.ap, a.ap), [a], [out])

    def mm(out, lhsT, rhs, start=True, stop=True):
        S.op("pe", lambda e: e.matmul(out.ap, lhsT.ap, rhs.ap, start=start, stop=stop), [lhsT, rhs], [out])

    def tr(out, in_):
        S.op("pe", lambda e: e.transpose(out.ap, in_.ap, identbf.ap), [in_, identbf], [out])

    def memset(eng, out, val):
        S.op(eng, lambda e: e.memset(out.ap, val), [], [out])

    PS = S.ps("psum", [128, 4096])
    PSB = PS.h.bitcast(BF16)
    bbuf = [Buf(f"bank{i}") for i in range(8)]

    def pv(i, c0=0, c1=512, p0=0, p1=128):
        return V(PS.h[p0:p1, i * 512 + c0:i * 512 + c1], (bbuf[i],))

    def pvb(i, c0, c1):
        return V(PSB[:, i * 1024 + c0:i * 1024 + c1], (bbuf[i],))

    def pvm(i0, n):
        return V(PS.h[:, i0 * 512:(i0 + n) * 512], tuple(bbuf[i0:i0 + n]))

    def pvmb(i0, n, c0, c1):
        return V(PSB[:, i0 * 1024 + c0:i0 * 1024 + c1], tuple(bbuf[i0:i0 + n]))

    rr = [0]

    def nb():
        i = rr[0]
        rr[0] = (rr[0] + 1) % 4
        return i

    def nb2():
        i = 0 if rr[0] < 2 else 2
        rr[0] = (i + 2) % 4
        return i

    G1 = 4
    g1_3d = lambda c0, c1: V(PS.h[:, G1 * 512:(G1 + 4) * 512].rearrange("p (c x) -> p c x", x=512)[:, :, c0:c1], tuple(bbuf[G1:G1 + 4]))

    consts = S.sb("consts", [128, CO_N], F32)
    pcol = S.sb("pcol", [128, PC_N], F32)
    cst = S.sb("cst", [128, 8], F32)
    omm = S.sb("omm", [128, 19], F32)
    hw0 = S.sb("hw0", [128, 4], F32)
    ha0 = S.sb("ha0", [128, 4], F32)
    hka = S.sb("hka", [128, 4], F32)
    omhka = S.sb("omhka", [128, 4], F32)
    identb = S.sb("identb", [128, 256], BF16)
    WmT = S.sb("WmT", [128, 512], BF16)
    WA = S.sb("WA", [128, 512], BF16)
    GU = S.sb("GU", [128, 512], BF16)
    GU2 = S.sb("GU2", [32, 512], BF16)
    bsrow = S.sb("bsrow", [1, 512], BF16)
    onesrow = S.sb("onesrow", [1, 128], BF16)
    onesb = S.sb("onesb", [128, NT], BF16)
    modc = S.sb("modc", [128, 64], F32)
    gt1t = S.sb("gt1t", [128, D], F32)
    cst_f = S.sb("cs_f", [128, 16], F32)

    maskall = consts[:, CO_MASK:CO_MASK + 512]
    resetm = consts[:, CO_RESET:CO_RESET + NT]
    bones = consts[:, CO_ONES:CO_ONES + 128]
    triu = consts[:, CO_TRIU:CO_TRIU + 128]
    ident32 = consts[:, CO_ID:CO_ID + 128]
    identbf = identb[:, 0:128]

    S.dma("sp", consts[:], CONST[:])
    S.dma("sp", pcol[:], PCOL[:])
    S.dma("pool", WA[:], WUA[:])
    S.dma("pool", GU[:], GUP[0:128, :])
    S.dma("pool", GU2[:], GUP[128:160, :])
    S.dma("pool", bsrow[:], BS[:])
    memset("pool", cst[:, 0:1], LN_EPS)
    memset("pool", cst[:, 1:2], GN_EPS)
    memset("pool", onesrow[:], 1.0)
    memset("pool", onesb[:], 1.0)
    cp("pool", identb[:, 0:128], ident32)
    cp("pool", identb[:, 128:256], ident32)
    ts("dve", omm[:], pcol[:, PC_MU:PC_MU + 19], -1.0, ALU.mult, 1.0, ALU.add)
    ts("dve", hw0[:], pcol[:, PC_W0:PC_W0 + 4], 0.5, ALU.mult)
    ts("dve", ha0[:], pcol[:, PC_A0:PC_A0 + 4], 0.5, ALU.mult)
    ts("dve", hka[:], pcol[:, PC_KA:PC_KA + 4], 0.5, ALU.mult)
    ts("dve", omhka[:], pcol[:, PC_KA:PC_KA + 4], -0.5, ALU.mult, 1.0, ALU.add)

    mk_w = S.mark()
    w_in = S.sb("w_in", [128, 8 * IN_COLS], BF16)
    w_out = S.sb("w_out", [128, 8 * D], BF16)
    mk_pro = S.mark()
    for kc in range(8):
        S.dma("pool", w_in[:, kc * IN_COLS:(kc + 1) * IN_COLS], WIN[kc * 128:(kc + 1) * 128, :])
    for kc in range(8):
        S.dma("pool", w_out[:, kc * D:(kc + 1) * D], WOUT[kc * 128:(kc + 1) * 128, :])
    wst32 = S.sb("wst32", [128, 512], F32)
    S.dma("sp", wst32[:], WST[:])
    for g in range(4):
        tt("dve", WmT[:, g * 128:(g + 1) * 128], wst32[:, g * 128:(g + 1) * 128], triu, ALU.mult)
    act(cst_f[:], pcol[:, PC_CT:PC_CT + 16], AF.Silu)
    csbc = [S.sb(f"csbc{b}", [128, 8 * 128], F32) for b in range(NSEQ)]
    for b in range(NSEQ):
        for kc in range(8):
            act(csbc[b][:, kc * 128:(kc + 1) * 128], pcol[:, PC_CT + b * 8 + kc:PC_CT + b * 8 + kc + 1].bc([128, 128]), AF.Silu)
    wa_st = [S.sb(f"wa_st{i}", [128, 8 * 1024], F32) for i in range(2)]
    brow = S.sb("brow", [128, 1024], F32)
    gttmp = S.sb("gttmp", [128, D], F32)
    colgrp = {0: 0, 1: 1, 3: 2, 4: 3}
    for g in range(6):
        st = wa_st[g % 2]
        for kc in range(8):
            S.dma("sp" if kc % 2 == 0 else "act", st[:, kc * 1024:(kc + 1) * 1024], WADA[kc * 128:(kc + 1) * 128, g * 1024:(g + 1) * 1024])
        if g in colgrp:
            bi = nb()
            for j in range(8):
                for kc in range(8):
                    mm(pv(bi, j * 2, j * 2 + 2), st[:, kc * 1024 + j * 128:kc * 1024 + (j + 1) * 128],
                       V(cst_f.h[:, :].rearrange("p (b k) -> p k b", b=2)[:, kc, :], (cst_f.buf,)), start=(kc == 0), stop=(kc == 7))
            q = colgrp[g]
            o3 = modc[:, q * 16:(q + 1) * 16].re("p (j b) -> p j b", b=2)
            i3 = pv(bi, 0, 16).re("p (j b) -> p j b", b=2)
            bb = V(pcol.h[:, PC_BADA + g * 8:PC_BADA + g * 8 + 8].unsqueeze(2).to_broadcast([128, 8, 2]), (pcol.buf,))
            tt("dve", o3, i3, bb, ALU.add)
        else:
            S.dma("sp", brow[:], V(BADA.h[0:1, g * 1024:(g + 1) * 1024].to_broadcast([128, 1024]), (BADA.buf,)))
            for b in range(NSEQ):
                for half in range(2):
                    bi = nb()
                    for kc in range(8):
                        mm(pv(bi), csbc[b][:, kc * 128:(kc + 1) * 128], st[:, kc * 1024 + half * 512:kc * 1024 + (half + 1) * 512],
                           start=(kc == 0), stop=(kc == 7))
                    tt("dve", gttmp[:, half * 512:(half + 1) * 512], pv(bi), brow[:, half * 512:(half + 1) * 512], ALU.add)
                S.dma("sp", (GT1S if g == 2 else GT2S)[b * 128:(b + 1) * 128, :], gttmp[:])
    ts("dve", modc[:, 16:32], modc[:, 16:32], 1.0, ALU.add)
    ts("dve", modc[:, 48:64], modc[:, 48:64], 1.0, ALU.add)

    def modcol(q, c, b):
        k = q * 16 + c * 2 + b
        return modc[:, k:k + 1]

    S.barrier()
    S.reset(mk_pro)

    rowsb = S.sb("rowsb", [128, 4 * D + 1024], F32)
    S.dma("sp", rowsb[:, 0:4 * D], V(ROWS.h[0:1, 0:4 * D].to_broadcast([128, 4 * D]), (ROWS.buf,)))
    S.dma("sp", rowsb[:, 4 * D:4 * D + 1024], V(ROWS.h[0:1, 6 * D:6 * D + 1024].to_broadcast([128, 1024]), (ROWS.buf,)))
    embg, embb = rowsb[:, 0:D], rowsb[:, D:2 * D]
    ln1g, ln1b = rowsb[:, 2 * D:3 * D], rowsb[:, 3 * D:4 * D]
    sgg, sgb = rowsb[:, 4 * D:4 * D + 512], rowsb[:, 4 * D + 512:4 * D + 1024]

    S32 = S.sb("S32", [128, 4 * 128], F32)
    S0b = S.sb("S0b", [128, 4 * 128], BF16)
    carry = S.sb("carry", [128, 12], F32)
    carryL = S.sb("carryL", [128, 3], F32)

    xt2 = [[S.sb(f"xt{p}_{j}", [128, D], F32) for j in range(NJ)] for p in range(2)]
    x0b = S.sb("x0b", [128, D], BF16)
    hT2 = [[S.sb(f"hT{p}_{c}", [128, NT], BF16) for c in range(8)] for p in range(2)]
    uT = [S.sb(f"uT{g}", [128, NT], BF16) for g in range(4)]
    vb = [S.sb(f"vb{j}", [128, 512], BF16) for j in range(NJ)]
    mixT = [S.sb(f"mixT{c}", [128, NT], BF16) for c in range(8)]
    Bmu = S.sb("Bmu", [128, NT + 1], F32)
    tw_p = [S.sb(f"tw{p}", [128, NT], BF16) for p in range(2)]
    sgt_p = [S.sb(f"sgt{p}", [128, NT], BF16) for p in range(2)]
    sgt2_p = [S.sb(f"sgt2{p}", [32, NT], BF16) for p in range(2)]
    lnscr = {k: (S.sb("stat" + k, [128, 12], F32), S.sb("mv" + k, [128, 2], F32), S.sb("rstd" + k, [128, 1], F32),
                 S.sb("nmr" + k, [128, 1], F32)) for k in ("x", "g", "o")}
    r1 = S.sb("r1", [128, D], F32)

    def ftile(name):
        return S.sb(name, [128, NT], F32)

    rS, kS, vS, th, tha = [ftile(n) for n in ("rS", "kS", "vS", "th", "tha")]
    cs_, eiP, sq, kk0, kkn = [ftile(n) for n in ("cs", "eiP", "sq", "kk0", "kkn")]
    f1, km, t1, rk = [ftile(n) for n in ("f1", "km", "t1", "rk")]
    ltmp = S.alias("ltmp", [128, NT], F32, [rk])
    gv = S.sb("gv", [128, 512], F32)
    ePs = S.sb("ePs", [128, NCH * 65], F32)
    eP3 = ePs[:].re("p (c t) -> p c t", t=65)
    bdBK = [[S.sb(f"bd{n}{i}", [128, NCH * 128], BF16) for n in ("B", "K", "V")] for i in range(2)]
    NCG = NCH // 2
    LtG = [S.sb(f"Lt{i}", [128, NCG * 512], BF16) for i in range(2)]
    ABoffG = [S.sb(f"ABoff{i}", [128, NCG * 256], BF16) for i in range(2)]
    bdA = [S.sb(f"bdA{h}", [128, NCH * 128], BF16) for h in range(4)]
    bdT = [S.sb(f"bdT{h}", [128, 3 * NCH * 128], BF16) for h in range(4)]
    rhat = [S.sb(f"rhat{h}", [128, NT], BF16) for h in range(4)]
    ABon = [S.sb(f"ABon{h}", [128, NCH * 256], BF16) for h in range(4)]
    Tt = [S.sb(f"Tt{h}", [128, NCH * 128], BF16) for h in range(4)]
    bonus = [S.sb(f"bonus{h}", [128, NT], BF16) for h in range(4)]
    gT = [S.sb(f"gT{h}", [128, NT], BF16) for h in range(4)]
    pcs = S.sb("pcs", [128, 4 * NCH], F32)
    RHSb = S.sb("RHSb", [128, 4 * 128], BF16)
    Ub = S.sb("Ub", [128, 4 * 128], BF16)

    for t_ in bdA + bdBK[0] + bdBK[1]:
        memset("pool", t_[:], 0.0)
    for c in range(NCH):
        memset("pool", ePs[:, c * 65:c * 65 + 1], 1.0)

    def layernorm_stats(src, width, key):
        stat, mv, rstd, nmr = lnscr[key]
        nchunk = width // 512
        for i in range(nchunk):
            a = src[:, i * 512:(i + 1) * 512]
            o = stat[:, i * 6:(i + 1) * 6]
            S.op("dve", lambda e, a=a, o=o: e.bn_stats(o.ap, a.ap), [a], [o])
        si = stat[:, 0:6 * nchunk]
        S.op("dve", lambda e: e.bn_aggr(mv[:].ap, si.ap), [si], [mv[:]])
        act(rstd[:], mv[:, 1:2], AF.Sqrt, bias=cst[:, 0:1], scale=1.0)
        recip(rstd[:], rstd[:])
        stt(nmr[:], mv[:, 0:1], -1.0, rstd[:], ALU.mult, ALU.mult)
        return rstd, nmr

    def proj_fm(col0, M, hT):
        bi = nb()
        for kc in range(8):
            mm(pv(bi, 0, NT, 0, M), w_in[:, kc * IN_COLS + col0:kc * IN_COLS + col0 + M], hT[kc][:], start=(kc == 0), stop=(kc == 7))
        return bi

    def shift_mix(bi, ci, out, p0=0, p1=128):
        cr, cc = (carry, ci) if ci < 12 else (carryL, ci - 12)
        ps = pv(bi, 0, NT, p0, p1)
        S.op("pool", lambda e: e.tensor_copy(Bmu[p0:p1, 0:1].ap, cr[p0:p1, cc:cc + 1].ap), [cr[:]], [Bmu[:]])
        act(Bmu[p0:p1, 1:NT + 1], ps, AF.Identity, scale=pcol[p0:p1, PC_MU + ci:PC_MU + ci + 1])
        stt(out, ps, omm[p0:p1, ci:ci + 1], Bmu[p0:p1, 0:NT], ALU.mult, ALU.add)
        S.op("pool", lambda e: e.tensor_copy(cr[p0:p1, cc:cc + 1].ap, Bmu[p0:p1, NT:NT + 1].ap), [Bmu[:]], [cr[:]])

    def halves(v):
        return [v[h * 64:(h + 1) * 64, :].re("p (c t) -> p c t", t=64) for h in range(2)]

    def bdhalves(tile):
        return [tile[h * 64:(h + 1) * 64, :].re("p (c t) -> p c t", t=128)[:, :, h * 64:(h + 1) * 64] for h in range(2)]

    def stage_x(b, tok0, par_):
        xt, hT = xt2[par_], hT2[par_]
        for j in range(NJ):
            S.dma("sp", xt[j][:], X[tok0 + j * 128:tok0 + (j + 1) * 128, :])
        for j in range(NJ):
            rstd, nmr = layernorm_stats(xt[j][:], D, "x")
            act(xt[j][:], xt[j][:], AF.Identity, bias=nmr[:], scale=rstd[:])
            yield
            tt("dve", xt[j][:], xt[j][:], embg, ALU.mult)
            tt("dve", xt[j][:], xt[j][:], embb, ALU.add)
            cp("act", x0b[:], xt[j][:])
            yield
            bi = nb()
            for c in range(8):
                tr(pvb(bi, c * 128, (c + 1) * 128), x0b[:, c * 128:(c + 1) * 128])
            for c in range(8):
                act(hT[c][:, j * 128:(j + 1) * 128], pvb(bi, c * 128, (c + 1) * 128), AF.Identity, bias=modcol(0, c, b), scale=modcol(1, c, b))
            yield

    def stage_lora(par_, first):
        hT = hT2[par_]
        tw, sgt, sgt2 = tw_p[par_], sgt_p[par_], sgt2_p[par_]
        if first:
            memset("pool", carryL[:], 0.0)
        bi = proj_fm(2560, 128, hT)
        shift_mix(bi, 12, ltmp[:], 0, 128)
        act(tw[0:64, :], ltmp[0:64, :], AF.Tanh)
        cp("pool", tw[64:128, :], ltmp[64:128, :])
        yield
        bi = proj_fm(2688, 128, hT)
        shift_mix(bi, 13, ltmp[:], 0, 128)
        act(sgt[:], ltmp[:], AF.Tanh, scale=0.5)
        yield
        bi = proj_fm(2816, 32, hT)
        shift_mix(bi, 14, ltmp[0:32, :], 0, 32)
        act(sgt2[:], ltmp[0:32, :], AF.Tanh, scale=0.5)
        yield

    def stage_gmlp_u(par_):
        hT = hT2[par_]
        for g in range(4):
            bi = proj_fm(g * 128, 128, hT)
            act(uT[g][:], pv(bi, 0, NT), AF.Gelu_apprx_tanh)
            yield

    def stage_gmlp(par_):
        hT = hT2[par_]
        for j in range(NJ):
            bi = nb()
            for kc in range(8):
                mm(pv(bi), hT[kc][:, j * 128:(j + 1) * 128], w_in[:, kc * IN_COLS + 512:kc * IN_COLS + 1024],
                   start=(kc == 0), stop=(kc == 7))
            act(gv[:], pv(bi), AF.Gelu_apprx_tanh)
            yield
            rstd, nmr = layernorm_stats(gv[:], 512, "g")
            act(gv[:], gv[:], AF.Identity, bias=nmr[:], scale=rstd[:])
            yield
            tt("dve", gv[:], gv[:], sgg, ALU.mult)
            tt("dve", vb[j][:], gv[:], sgb, ALU.add)
            yield
        for g in range(4):
            bi = nb()
            for j in range(NJ):
                o = pv(bi, j * 128, (j + 1) * 128)
                mm(o, vb[j][:, g * 128:(g + 1) * 128], WmT[:, g * 128:(g + 1) * 128], start=True, stop=False)
                mm(o, onesrow[0:1, :], bsrow[0:1, g * 128:(g + 1) * 128], start=False, stop=True)
            tt("dve", mixT[g][:], pv(bi, 0, NT), uT[g][:], ALU.mult)
            yield

    def prep_R(hp, par_, first):
        hT = hT2[par_]
        tw, sgt, sgt2 = tw_p[par_], sgt_p[par_], sgt2_p[par_]
        hs = slice(hp * 128, (hp + 1) * 128)
        bi = proj_fm(1024 + hp * 128, 128, hT)
        shift_mix(bi, hp, rS[:])
        yield
        bi = nb()
        mm(pv(bi, 0, NT), WA[0:64, hs], tw[0:64, :])
        act(th[:], pv(bi, 0, NT), AF.Tanh, bias=hw0[:, hp:hp + 1], scale=0.5)
        ts("dve", th[:], th[:], 0.5, ALU.mult, 0.5, ALU.add)
        yield
        S.op("dve", lambda e: e.tensor_tensor_scan(cs_[:].ap, resetm.ap, th[:].ap, 0.0, ALU.mult, ALU.add),
             [resetm, th[:]], [cs_[:]])
        cs3 = cs_[:].re("p (c t) -> p c t", t=64)
        act(eP3[:, :, 1:65], cs3, AF.Exp, scale=-C0)
        act(eiP[:], cs_[:], AF.Exp, scale=C0)
        cp("pool", pcs[:, hp * NCH:(hp + 1) * NCH].re("p (c o) -> p c o", o=1), eP3[:, :, 64:65])
        yield
        bi = proj_fm(2048 + hp * 128, 128, hT)
        shift_mix(bi, 8 + hp, vS[:])
        yield
        bi = nb()
        mm(pv(bi, 0, NT), GU[:, hs], sgt[:], start=True, stop=False)
        mm(pv(bi, 0, NT), GU2[0:32, hs], sgt2[0:32, :], start=False, stop=False)
        mm(pv(bi, 0, NT), GU[:, hs], onesb[:], start=False, stop=False)
        mm(pv(bi, 0, NT), GU2[0:32, hs], onesb[0:32, :], start=False, stop=True)
        act(gT[hp][:], pv(bi, 0, NT), AF.Identity, scale=0.5)
        yield
        tt("dve", rhat[hp][:].re("p (c t) -> p c t", t=64), rS[:].re("p (c t) -> p c t", t=64), eP3[:, :, 1:65], ALU.mult)
        oV = bdhalves(bdBK[hp % 2][2])
        vSh = halves(vS[:])
        for h in range(2):
            cp("act", oV[h], vSh[h])
        yield

    def prep_K(hp, par_, first):
        hT = hT2[par_]
        tw = tw_p[par_]
        hs = slice(hp * 128, (hp + 1) * 128)
        if first and hp == 0:
            memset("pool", carry[:], 0.0)
        bi = proj_fm(1536 + hp * 128, 128, hT)
        shift_mix(bi, 4 + hp, kS[:])
        yield
        bi = nb()
        mm(pv(bi, 0, NT), WA[64:128, hs], tw[64:128, :])
        act(tha[:], pv(bi, 0, NT), AF.Tanh, bias=ha0[:, hp:hp + 1], scale=0.5)
        yield
        ts("dve", kk0[:], kS[:], pcol[:, PC_KK + hp:PC_KK + hp + 1], ALU.mult)
        act(sq[:], kS[:], AF.Square, scale=pcol[:, PC_KK + hp:PC_KK + hp + 1])
        yield
        bi = nb()
        mm(pv(bi, 0, NT), bones, sq[:])
        ts("dve", sq[:], pv(bi, 0, NT), 1e-24, ALU.max)
        yield
        recip(sq[:], sq[:])
        tt("dve", kkn[:], kk0[:], sq[:], ALU.mult)
        yield
        ts("dve", f1[:], tha[:], hka[:, hp:hp + 1], ALU.mult, omhka[:, hp:hp + 1], ALU.add)
        tt("dve", km[:], kS[:], f1[:], ALU.mult)
        stt(t1[:], tha[:], 1.0, kk0[:], ALU.add, ALU.mult)
        yield

    def prep_J(hp):
        bdB, bdK, bdV = bdBK[hp % 2]
        kknh, t1h, eiPh, kmh = halves(kkn[:]), halves(t1[:]), halves(eiP[:]), halves(km[:])
        oA, oB, oK = bdhalves(bdA[hp]), bdhalves(bdB), bdhalves(bdK)
        for h in range(2):
            ePm_h = eP3[h * 64:(h + 1) * 64, :, 0:64]
            stt(oA[h], kknh[h], -0.5, ePm_h, ALU.mult, ALU.mult)
            tt("dve", oB[h], t1h[h], eiPh[h], ALU.mult)
            tt("dve", oK[h], kmh[h], eiPh[h], ALU.mult)
            yield
        stt(rk[:], rS[:], pcol[:, PC_RK + hp:PC_RK + hp + 1], km[:], ALU.mult, ALU.mult)
        bi = nb()
        mm(pv(bi, 0, NT), bones, rk[:])
        tt("dve", bonus[hp][:], pv(bi, 0, NT), vS[:], ALU.mult)
        yield
        bp = nb2()
        for n_, src in enumerate((bdB, bdK, bdV)):
            for c in range(NCH):
                k0 = (n_ * NCH + c) * 128
                tr(pvmb(bp, 2, k0, k0 + 128), src[:, c * 128:(c + 1) * 128])
        cp("act", bdT[hp][:], pvmb(bp, 2, 0, 3 * NCH * 128))
        yield

    def stage_prep(hp, par_, first):
        return seq(par(prep_K(hp, par_, first), prep_R(hp, par_, first)), prep_J(hp))

    def stage_T(hp, cg):
        bdB, bdK, bdV = bdBK[hp % 2]
        Lt, ABo = LtG[cg], ABoffG[cg]
        sl = lambda t_, c: t_[:, c * 128:(c + 1) * 128]
        B0 = G1 + cg * NCG
        cbufs = tuple(bbuf[B0:B0 + NCG])
        g3 = lambda c0, c1: V(PS.h[:, B0 * 512:(B0 + NCG) * 512].rearrange("p (c x) -> p c x", x=512)[:, :, c0:c1], cbufs)
        for i in range(NCG):
            c = cg * NCG + i
            rh = rhat[hp][:, c * 64:(c + 1) * 64]
            mm(pv(B0 + i, 0, 128), sl(bdB, c), sl(bdA[hp], c))
            mm(pv(B0 + i, 128, 256), sl(bdA[hp], c), sl(bdB, c))
            mm(pv(B0 + i, 256, 384), sl(bdK, c), sl(bdA[hp], c))
            mm(pv(B0 + i, 384, 448), sl(bdB, c), rh)
            mm(pv(B0 + i, 448, 512), sl(bdK, c), rh)
        abo3 = ABo[:].re("p (c x) -> p c x", x=256)
        abn3 = ABon[hp][:, cg * NCG * 256:(cg + 1) * NCG * 256].re("p (c x) -> p c x", x=256)
        mk3a = V(consts.h[:, CO_MASK:CO_MASK + 256].unsqueeze(1).to_broadcast([128, NCG, 256]), (consts.buf,))
        mk3b = V(consts.h[:, CO_MASK + 256:CO_MASK + 512].unsqueeze(1).to_broadcast([128, NCG, 256]), (consts.buf,))
        tt("dve", abo3, g3(0, 256), mk3a, ALU.mult)
        tt("dve", abn3, g3(256, 512), mk3b, ALU.mult)
        yield
        l3 = Lt[:].re("p (c x) -> p c x", x=512)
        id3 = V(identb.h[:, 0:128].unsqueeze(1).to_broadcast([128, NCG, 128]), (identb.buf,))
        tt("dve", l3[:, :, 0:128], abo3[:, :, 0:128], id3, ALU.add)
        for i in range(NCG):
            M_, MT_ = ABo[:, i * 256:i * 256 + 128], ABo[:, i * 256 + 128:i * 256 + 256]
            mm(pv(B0 + i, 128, 256), MT_, M_)
            mm(pv(B0 + i, 256, 384), M_, MT_)
        cp("act", l3[:, :, 128:384], g3(128, 384))
        yield
        for k in range(1, 5):
            for i in range(NCG):
                o = i * 512
                Q, P, PT = Lt[:, o:o + 128], Lt[:, o + 128:o + 256], Lt[:, o + 256:o + 384]
                mm(pv(B0 + i, 0, 128), PT, Q)
                mm(pv(B0 + i, 128, 256), PT, P)
                mm(pv(B0 + i, 256, 384), P, PT)
            tt("dve", l3[:, :, 0:128], g3(0, 128), l3[:, :, 0:128], ALU.add)
            cp("act", l3[:, :, 128:384], g3(128, 384))
            yield
        for i in range(NCG):
            o = i * 512
            mm(pv(B0, i * 128, (i + 1) * 128), Lt[:, o + 256:o + 384], Lt[:, o:o + 128])
        tt("dve", Tt[hp][:, cg * NCG * 128:(cg + 1) * NCG * 128].re("p (c x) -> p c x", x=128),
           pv(B0, 0, NCG * 128).re("p (c x) -> p c x", x=128), l3[:, :, 0:128], ALU.add)
        yield

    def stage_onpath():
        sl = lambda t_, c: t_[:, c * 128:(c + 1) * 128]
        hsl = lambda t_, hp: t_[:, hp * 128:(hp + 1) * 128]
        for c in range(NCH):
            bi = nb()
            for hp in range(4):
                o = pv(bi, hp * 128, (hp + 1) * 128)
                mm(o, sl(bdA[hp], c), hsl(S0b, hp), start=True, stop=False)
                mm(o, ABon[hp][:, c * 256:c * 256 + 128], bdT[hp][:, (2 * NCH + c) * 128:(2 * NCH + c + 1) * 128], start=False, stop=True)
            cp("act", RHSb[:], pv(bi))
            yield
            bi = nb()
            for hp in range(4):
                mm(pv(bi, hp * 128, (hp + 1) * 128), sl(Tt[hp], c), hsl(RHSb, hp))
            cp("dve", Ub[:], pv(bi))
            yield
            bi = nb()
            for hp in range(4):
                o = pv(bi, hp * 128, (hp + 1) * 128)
                Bt_c = bdT[hp][:, c * 128:(c + 1) * 128]
                Kt_c = bdT[hp][:, (NCH + c) * 128:(NCH + c + 1) * 128]
                Vt_c = bdT[hp][:, (2 * NCH + c) * 128:(2 * NCH + c + 1) * 128]
                mm(o, ident32, hsl(S32, hp), start=True, stop=False)
                mm(o, Bt_c, hsl(Ub, hp), start=False, stop=False)
                mm(o, Kt_c, Vt_c, start=False, stop=True)
            for hp in range(4):
                Vt_c = bdT[hp][:, (2 * NCH + c) * 128:(2 * NCH + c + 1) * 128]
                rh = rhat[hp][:, c * 64:(c + 1) * 64]
                yo = pv(G1 + hp, c * 64, (c + 1) * 64)
                mm(yo, hsl(S0b, hp), rh, start=True, stop=False)
                mm(yo, hsl(Ub, hp), ABon[hp][:, c * 256 + 128:c * 256 + 192], start=False, stop=False)
                mm(yo, Vt_c, ABon[hp][:, c * 256 + 192:c * 256 + 256], start=False, stop=True)
            pc3 = V(pcs.h[:, :].rearrange("p (h c) -> p h c", c=NCH)[:, :, c:c + 1].to_broadcast([128, 4, 128]), (pcs.buf,))
            s3 = lambda t_: t_[:].re("p (h x) -> p h x", x=128)
            tt("dve", s3(S0b), pv(bi).re("p (h x) -> p h x", x=128), pc3, ALU.mult)
            tt("dve", s3(S32), pv(bi).re("p (h x) -> p h x", x=128), pc3, ALU.mult)
            yield

    gn_tmp = [(rS, kS, vS), (th, tha, cs_), (eiP, sq, kk0), (kkn, f1, km)]

    def stage_gn(hp):
        Y, Ysq, Z = gn_tmp[hp]
        yb = pv(G1 + hp, 0, NT)
        cp("act", Y[:], yb)
        act(Ysq[:], yb, AF.Square)
        yield
        b1 = nb()
        mm(pv(b1, 0, NT), bones, Y[:])
        b2 = nb()
        mm(pv(b2, 0, NT), bones, Ysq[:])
        stt(Z[:], pv(b1, 0, NT), -1.0 / 64, Y[:], ALU.mult, ALU.add)
        act(Y[:], pv(b1, 0, NT), AF.Square, scale=1.0 / 64)
        stt(Ysq[:], pv(b2, 0, NT), 1.0 / 64, Y[:], ALU.mult, ALU.subtract)
        yield
        act(Ysq[:], Ysq[:], AF.Sqrt, bias=cst[:, 1:2], scale=1.0)
        recip(Ysq[:], Ysq[:])
        yield
        tt("dve", Z[:], Z[:], Ysq[:], ALU.mult)
        act(Z[:], Z[:], AF.Identity, bias=pcol[:, PC_LB + hp:PC_LB + hp + 1], scale=pcol[:, PC_LG + hp:PC_LG + hp + 1])
        yield
        tt("dve", Z[:], Z[:], bonus[hp][:], ALU.add)
        tt("dve", mixT[4 + hp][:], Z[:], gT[hp][:], ALU.mult)
        yield

    def stage_out(b, tok0, par_, first):
        xt = xt2[par_]
        if first:
            S.dma("sp", gt1t[:], GT1S[b * 128:(b + 1) * 128, :])
        for j in range(NJ):
            for half in range(2):
                bi = nb()
                for kc in range(8):
                    mm(pv(bi), mixT[kc][:, j * 128:(j + 1) * 128], w_out[:, kc * D + half * 512:kc * D + (half + 1) * 512],
                       start=(kc == 0), stop=(kc == 7))
                tt("dve", r1[:, half * 512:(half + 1) * 512], pv(bi), gt1t[:, half * 512:(half + 1) * 512], ALU.mult)
                yield
            stt(r1[:], xt[j][:], ALPHA, r1[:], ALU.mult, ALU.add)
            rstd, nmr = layernorm_stats(r1[:], D, "o")
            act(r1[:], r1[:], AF.Identity, bias=nmr[:], scale=rstd[:])
            yield
            tt("dve", r1[:], r1[:], ln1g, ALU.mult)
            tt("dve", r1[:], r1[:], ln1b, ALU.add)
            blk = (tok0 + j * 128) // 128
            S.dma("sp", X1SB[blk][tok0 + j * 128:tok0 + (j + 1) * 128, :], r1[:])
            yield

    steps = [(b, s) for b in range(NSEQ) for s in range(NSTEP)]

    def stage_B(n):
        b, s = steps[n]
        return stage_x(b, b * SEQ + s * NT, n % 2)

    def stage_L(n):
        b, s = steps[n]
        return stage_lora(n % 2, s == 0)

    def reset_state():
        memset("pool", S32[:], 0.0)
        memset("pool", S0b[:], 0.0)
        yield

    TG = lambda name, g: tagged(S, name, g)
    drive(TG("0.B", stage_B(0)))
    drive(TG("0.lora", stage_L(0)))
    drive(TG("0.prep0", stage_prep(0, 0, True)))
    for n, (b, s) in enumerate(steps):
        tok0 = b * SEQ + s * NT
        pn = n % 2
        sid = f"{n}."
        if s == 0:
            drive(reset_state())
        for hp in range(4):
            gens = [TG(sid + f"T{hp}a", stage_T(hp, 0)), TG(sid + f"T{hp}b", stage_T(hp, 1))]
            if hp < 3:
                gens.append(TG(sid + f"prep{hp + 1}", stage_prep(hp + 1, pn, s == 0)))
            else:
                gens.append(TG(sid + "gmlp", stage_gmlp(pn)))
            if hp == 2:
                gens.append(TG(sid + "gmlpu", stage_gmlp_u(pn)))
            drive(par(*gens))
        gens = [seq(TG(sid + "onpath", stage_onpath()), par(*[TG(sid + f"gn{hp}", stage_gn(hp)) for hp in range(4)]))]
        if n + 1 < len(steps):
            gens.append(seq(TG(f"{n + 1}.B", stage_B(n + 1)), TG(f"{n + 1}.lora", stage_L(n + 1))))
        drive(par(*gens))
        gens = [TG(sid + "out", stage_out(b, tok0, pn, s == 0))]
        if n + 1 < len(steps):
            gens.append(TG(f"{n + 1}.prep0", stage_prep(0, (n + 1) % 2, steps[n + 1][1] == 0)))
        drive(par(*gens))

    S.barrier()
    S.reset(mk_w)
    NBLKF = FFN // 256
    w_fg = [S.sb(f"w_fg{i}", [128, 8 * 256], BF16) for i in range(NBLKF)]
    w_fu = [S.sb(f"w_fu{i}", [128, 8 * 256], BF16) for i in range(NBLKF)]
    w_fo = [S.sb(f"w_fo{k}", [128, D], BF16) for k in range(NKF)]
    wfi3 = WFI.h[:, :].rearrange("(k p) n -> p k n", p=128)
    for i in range(NBLKF):
        S.dma("pool", w_fg[i][:].re("p (k n) -> p k n", n=256), V(wfi3[:, :, i * 256:(i + 1) * 256], (WFI.buf,)))
        S.dma("pool", w_fu[i][:].re("p (k n) -> p k n", n=256), V(wfi3[:, :, FFN + i * 256:FFN + (i + 1) * 256], (WFI.buf,)))
    for kc in range(NKF):
        S.dma("pool", w_fo[kc][:], WFO[kc * 128:(kc + 1) * 128, :])
    rows2 = S.sb("rows2", [128, 2 * D], F32)
    S.dma("sp", rows2[:], V(ROWS.h[0:1, 4 * D:6 * D].to_broadcast([128, 2 * D]), (ROWS.buf,)))
    ln2g, ln2b = rows2[:, 0:D], rows2[:, D:2 * D]
    gt2t = S.sb("gt2t", [128, D], F32)
    NT2 = 512
    NJ2 = NT2 // 128
    xs = S.sb("xs", [128, D], F32)
    xr = S.sb("xr", [128, D], F32)
    x1b = S.sb("x1b", [128, D], BF16)
    h2 = [S.sb(f"h2_{c}", [128, NT2], BF16) for c in range(8)]
    aT = [S.sb(f"aT{c}", [128, NT2], BF16) for c in range(NKF)]
    sgl = S.sb("sgl", [128, NT2], BF16)
    r2 = S.sb("r2", [128, D], F32)
    stat2 = S.sb("stat2", [128, 12], F32)
    mv2 = S.sb("mv2", [128, 2], F32)
    rstd2 = S.sb("rstd2", [128, 1], F32)
    nmr2 = S.sb("nmr2", [128, 1], F32)
    rr8 = [0]

    def nb8():
        i = rr8[0]
        rr8[0] = (rr8[0] + 1) % 8
        return i

    steps2 = [(b, s) for b in range(NSEQ) for s in range(SEQ // NT2)]

    def ffn_A(n):
        b, s = steps2[n]
        tok0 = b * SEQ + s * NT2
        for j in range(NJ2):
            blk = (tok0 + j * 128) // 128
            S.dma("sp", xs[:], X1SB[blk][tok0 + j * 128:tok0 + (j + 1) * 128, :])
            cp("act", x1b[:], xs[:])
            yield
            bi = nb8()
            for c in range(8):
                tr(pvb(bi, c * 128, (c + 1) * 128), x1b[:, c * 128:(c + 1) * 128])
            for c in range(8):
                act(h2[c][:, j * 128:(j + 1) * 128], pvb(bi, c * 128, (c + 1) * 128), AF.Identity, bias=modcol(2, c, b), scale=modcol(3, c, b))
            yield

    def ffn_B(n):
        for m_ in range(NKF):
            bg = nb8()
            for kc in range(8):
                mm(pv(bg), w_fg[m_ // 2][:, kc * 256 + (m_ % 2) * 128:kc * 256 + (m_ % 2) * 128 + 128], h2[kc][:], start=(kc == 0), stop=(kc == 7))
            bu = nb8()
            for kc in range(8):
                mm(pv(bu), w_fu[m_ // 2][:, kc * 256 + (m_ % 2) * 128:kc * 256 + (m_ % 2) * 128 + 128], h2[kc][:], start=(kc == 0), stop=(kc == 7))
            act(sgl[:], pv(bg), AF.Silu)
            tt("dve", aT[m_][:], pv(bu), sgl[:], ALU.mult)
            yield

    def ffn_C(n):
        b, s = steps2[n]
        tok0 = b * SEQ + s * NT2
        if s == 0:
            S.dma("sp", gt2t[:], GT2S[b * 128:(b + 1) * 128, :])
        for j in range(NJ2):
            blk = (tok0 + j * 128) // 128
            S.dma("sp", xr[:], X1SB[blk][tok0 + j * 128:tok0 + (j + 1) * 128, :])
            for half in range(2):
                bi = nb8()
                for kc in range(NKF):
                    mm(pv(bi), aT[kc][:, j * 128:(j + 1) * 128], w_fo[kc][:, half * 512:(half + 1) * 512],
                       start=(kc == 0), stop=(kc == NKF - 1))
                tt("dve", r2[:, half * 512:(half + 1) * 512], pv(bi), gt2t[:, half * 512:(half + 1) * 512], ALU.mult)
                yield
            stt(r2[:], xr[:], ALPHA, r2[:], ALU.mult, ALU.add)
            for i in range(2):
                a = r2[:, i * 512:(i + 1) * 512]
                o = stat2[:, i * 6:(i + 1) * 6]
                S.op("dve", lambda e, a=a, o=o: e.bn_stats(o.ap, a.ap), [a], [o])
            S.op("dve", lambda e: e.bn_aggr(mv2[:].ap, stat2[:].ap), [stat2[:]], [mv2[:]])
            act(rstd2[:], mv2[:, 1:2], AF.Sqrt, bias=cst[:, 0:1], scale=1.0)
            recip(rstd2[:], rstd2[:])
            stt(nmr2[:], mv2[:, 0:1], -1.0, rstd2[:], ALU.mult, ALU.mult)
            yield
            act(r2[:], r2[:], AF.Identity, bias=nmr2[:], scale=rstd2[:])
            tt("dve", r2[:], r2[:], ln2g, ALU.mult)
            tt("dve", r2[:], r2[:], ln2b, ALU.add)
            S.dma("sp", OUTB[blk][tok0 + j * 128:tok0 + (j + 1) * 128, :], r2[:])
            yield

    drive(ffn_A(0))
    for n in range(len(steps2)):
        drive(ffn_B(n))
        gens = [ffn_C(n)]
        if n + 1 < len(steps2):
            gens.append(ffn_A(n + 1))
        drive(par(*gens))

    S.emit_all()
    return nc, S


def _consts():
    c = np.zeros((128, CO_N), np.float32)
    p = np.arange(128)
    hb = p // 64
    pi = p % 64
    col = np.arange(128)
    ch, ci = col // 64, col % 64
    same = hb[:, None] == ch[None, :]
    up = same & (pi[:, None] < ci[None, :])
    lo = same & (pi[:, None] > ci[None, :])
    c[:, 0:128] = up
    c[:, 128:256] = lo
    c[:, 256:384] = up
    t = np.arange(64)
    le = (pi[:, None] <= t[None, :])
    c[:, 384:448] = le
    c[:, 448:512] = le
    r = np.ones((128, 256), np.float32)
    r[:, ::64] = 0.0
    c[:, CO_RESET:CO_RESET + 256] = r
    c[:, CO_ID:CO_ID + 128] = np.eye(128)
    c[:, CO_ONES:CO_ONES + 128] = same
    c[:, CO_TRIU:CO_TRIU + 128] = (p[:, None] <= col[None, :])
    return c


def _colchunks(v, n):
    return np.ascontiguousarray(v.reshape(n, 128).T)


_CACHE = {}


def kernel(x, c, emb_ln_g, emb_ln_b, w_ada, b_ada, w_in, mu_shift, sg_ln_g, sg_ln_b, w_s, b_s,
           w0, w_up, a0, a_up, g_up, k_k, k_a, r_k, lnx_g, lnx_b, w_out, ln1_g, ln1_b,
           w_ffn_in, w_ffn_out, ln2_g, ln2_b):
    f = lambda a: np.ascontiguousarray(np.asarray(a, dtype=np.float32))
    x, c = f(x), f(c)
    if "nc" not in _CACHE:
        _CACHE["nc"] = build_program()
    nc, S = _CACHE["nc"]
    mu = f(mu_shift)[0]
    pc = np.zeros((128, PC_N), np.float32)
    pc[:, 0:12] = _colchunks(mu[0:1536], 12)
    pc[:, 12] = mu[1536:1664]
    pc[:, 13] = mu[1664:1792]
    pc[0:32, 14] = mu[1792:1824]
    pc[:, PC_W0:PC_W0 + 4] = _colchunks(f(w0)[0], 4)
    pc[:, PC_A0:PC_A0 + 4] = _colchunks(f(a0)[0], 4)
    pc[:, PC_KK:PC_KK + 4] = _colchunks(f(k_k)[0], 4)
    pc[:, PC_KA:PC_KA + 4] = _colchunks(f(k_a)[0], 4)
    pc[:, PC_RK:PC_RK + 4] = _colchunks(f(r_k)[0].reshape(-1), 4)
    pc[:, PC_LG:PC_LG + 4] = _colchunks(f(lnx_g)[0], 4)
    pc[:, PC_LB:PC_LB + 4] = _colchunks(f(lnx_b)[0], 4)
    pc[:, PC_BADA:PC_BADA + 48] = _colchunks(f(b_ada)[0], 48)
    rows = np.concatenate([f(emb_ln_g), f(emb_ln_b), f(ln1_g)[0], f(ln1_b)[0], f(ln2_g)[0], f(ln2_b)[0],
                           f(sg_ln_g)[0], f(sg_ln_b)[0]])[None, :]
    w_sT = np.ascontiguousarray(np.transpose(f(w_s)[0], (2, 0, 1)).reshape(128, 512))
    w_ua = np.ascontiguousarray(np.concatenate([f(w_up)[0], f(a_up)[0]], axis=0))
    common = {
        "consts": _consts(), "w_ada": f(w_ada)[0], "b_ada": f(b_ada), "w_in": f(w_in)[0], "w_out": f(w_out)[0],
        "w_ffn_in": f(w_ffn_in)[0], "w_ffn_out": f(w_ffn_out)[0], "rows": np.ascontiguousarray(rows),
        "b_s": f(b_s)[0].reshape(1, 512), "w_sT": w_sT, "w_ua": w_ua, "g_up": f(g_up)[0],
    }
    in_maps = []
    for i in range(8):
        pci = pc.copy()
        for b in range(NSEQ):
            pci[:, PC_CT + b * 8:PC_CT + (b + 1) * 8] = _colchunks(c[i * NSEQ + b], 8)
        m = dict(common)
        m["pcol"] = pci
        m["x"] = np.ascontiguousarray(x[i * NSEQ:(i + 1) * NSEQ].reshape(NSEQ * SEQ, D))
        in_maps.append(m)
    res = run_bass_kernel_spmd(nc, in_maps, core_ids=list(range(8)))
    out = np.concatenate([r["out"].reshape(NSEQ, SEQ, D) for r in res.results], axis=0)
    return out.astype(np.float32)
```
